# Optimizing a Trainium2 kernel written in Bass

```python
import math
import jax, jax.numpy as jnp
from jax import lax
import numpy as np

D_MODEL = 1024
BATCH = 8
SEQ = 2048
DEPTH = 2
DEC_BATCH = 32
DEC_SEQ = 16
PAST_LEN = 1024

CHUNK = 64
Q_BLOCK = 128
EPS = 1e-6
D_MIX = 1024
GLA_HEADS = 4
GLA_DV = 64
GLA_DK = 32
GLA_WIDTH = GLA_HEADS * GLA_DV
GLA_GATE_RANK = 16
GLA_GATE_TAU = 16.0
MLA_HEADS = 4
MLA_NOPE_DIM = 64
MLA_ROPE_DIM = 32
MLA_V_DIM = 128
MLA_Q_RANK = 192
MLA_KV_RANK = 128
MLA_WIDTH = MLA_HEADS * MLA_V_DIM
ROPE_BASE = 10000.0
S5_GROUPS = 16
S5_GROUP_CH = 16
S5_STATE = 64
S5_WIDTH = S5_GROUPS * S5_GROUP_CH
DT_MIN = 1e-3
DT_MAX = 1e-1
SPLIT_SIZES = (GLA_HEADS * GLA_DK, GLA_HEADS * GLA_DK, GLA_WIDTH, GLA_GATE_RANK, GLA_WIDTH,
               MLA_Q_RANK, MLA_KV_RANK, MLA_ROPE_DIM, MLA_WIDTH,
               S5_WIDTH, S5_WIDTH)
IN_COLS = sum(SPLIT_SIZES)

kernel_name = 'hybrid_gla_mla_s5_stream_step'


def rmsnorm(x, g):
    xf = x.astype(jnp.float32)
    y = xf * lax.rsqrt(jnp.mean(xf * xf, axis=-1, keepdims=True) + EPS)
    return (y * g.astype(jnp.float32)).astype(x.dtype)


def apply_rope(x, pos):
    half = x.shape[-1] // 2
    inv = ROPE_BASE ** (-jnp.arange(half, dtype=jnp.float32) / half)
    ang = pos.astype(jnp.float32)[:, None] * inv[None, :]
    ang = ang.reshape(ang.shape[:1] + (1,) * (x.ndim - 3) + ang.shape[1:])
    cos, sin = jnp.cos(ang), jnp.sin(ang)
    xf = x.astype(jnp.float32)
    x1, x2 = xf[..., :half], xf[..., half:]
    return jnp.concatenate([x1 * cos - x2 * sin, x1 * sin + x2 * cos], axis=-1).astype(x.dtype)


def gla_recurrence(q, k, v, log_a, s0):
    f32 = jnp.float32
    bsz, t = q.shape[:2]
    L = min(CHUNK, t)
    n = t // L

    def to_chunks(a):
        return jnp.moveaxis(a.astype(f32).reshape((bsz, n, L) + a.shape[2:]), 1, 0)

    causal = jnp.tril(jnp.ones((L, L), dtype=bool))[None, :, :, None, None]

    def step(S, inp):
        qc, kc, vc, gc = inp
        b = jnp.cumsum(gc, axis=1)
        diff = b[:, :, None] - b[:, None, :]
        decay = jnp.exp(jnp.where(causal, diff, -jnp.inf))
        scores = jnp.sum(qc[:, :, None] * kc[:, None] * decay, axis=-1)
        o = (jnp.einsum('bijh,bjhv->bihv', scores, vc)
             + jnp.einsum('bihk,bhkv->bihv', qc * jnp.exp(b), S))
        b_last = b[:, -1]
        S = (jnp.exp(b_last)[..., None] * S
             + jnp.einsum('bjhk,bjhv->bhkv', kc * jnp.exp(b_last[:, None] - b), vc))
        return S, o

    S, o = lax.scan(step, s0.astype(f32), (to_chunks(q), to_chunks(k), to_chunks(v), to_chunks(log_a)))
    o = jnp.moveaxis(o, 0, 1).reshape((bsz, t) + o.shape[3:])
    return o, S


def chunk_causal_attention(q_nope, q_pe, q_pos, k_nope, k_pe, v, k_pos):
    bsz, t, nh, _ = q_nope.shape
    qb = min(Q_BLOCK, t)
    nb = t // qb
    scale = (MLA_NOPE_DIM + MLA_ROPE_DIM) ** -0.5
    k_chunk = k_pos // CHUNK

    def blocks(a):
        return jnp.moveaxis(a.reshape((bsz, nb, qb) + a.shape[2:]), 1, 0)

    def one_block(args):
        qn, qp, qpos = args
        s = (jnp.einsum('bqhd,bshd->bhqs', qn, k_nope)
             + jnp.einsum('bqhr,bsr->bhqs', qp, k_pe)).astype(jnp.float32) * scale
        mask = k_chunk[None, :] <= (qpos // CHUNK)[:, None]
        s = jnp.where(mask[None, None], s, -jnp.inf)
        p = jax.nn.softmax(s, axis=-1).astype(v.dtype)
        return jnp.einsum('bhqs,bshv->bqhv', p, v)

    o = lax.map(one_block, (blocks(q_nope), blocks(q_pe), q_pos.reshape(nb, qb)))
    return jnp.moveaxis(o, 0, 1).reshape(bsz, t, nh, v.shape[-1])


def s5_scan(u, lam_re, lam_im, b_re, b_im, c_re, c_im, d, log_dt, x0_re, x0_im):
    f32 = jnp.float32
    bsz, t, _ = u.shape
    uf = u.astype(f32).reshape(bsz, t, S5_GROUPS, S5_GROUP_CH)
    lam = lax.complex(lam_re.astype(f32), lam_im.astype(f32))
    dt = jnp.exp(log_dt.astype(f32))[:, None]
    lam_bar = jnp.exp(lam * dt)
    b_bar = ((lam_bar - 1.0) / lam)[:, :, None] * lax.complex(b_re.astype(f32), b_im.astype(f32))
    bu = jnp.einsum('btgh,gph->btgp', uf.astype(jnp.complex64), b_bar)
    x0 = lax.complex(x0_re.astype(f32), x0_im.astype(f32))
    bu = bu.at[:, 0].add(lam_bar * x0)
    a = jnp.broadcast_to(lam_bar, bu.shape)

    def combine(e1, e2):
        a1, b1 = e1
        a2, b2 = e2
        return a1 * a2, a2 * b1 + b2

    _, xs = lax.associative_scan(combine, (a, bu), axis=1)
    c = lax.complex(c_re.astype(f32), c_im.astype(f32))
    y = jnp.real(jnp.einsum('btgp,ghp->btgh', xs, c)) + d.astype(f32).reshape(S5_GROUPS, S5_GROUP_CH) * uf
    x_last = xs[:, -1]
    return y.reshape(bsz, t, S5_WIDTH), jnp.real(x_last), jnp.imag(x_last)


def hybrid_layer(h, gla_s0, ckv_past, kpe_past, s5_x0_re, s5_x0_im,
                 w_in, gla_w_gate, gla_b_gate, gla_norm_gain,
                 mla_q_norm_gain, mla_w_uq, mla_kv_norm_gain, mla_w_ukv,
                 s5_lambda_re, s5_lambda_im, s5_b_re, s5_b_im, s5_c_re, s5_c_im,
                 s5_d, s5_log_dt, s5_w_glu, s5_b_glu, w_out):
    f32 = jnp.float32
    bsz, t, _ = h.shape
    past = ckv_past.shape[1]
    q_pos = past + jnp.arange(t)
    k_pos = jnp.arange(past + t)
    points = np.cumsum(SPLIT_SIZES)[:-1].tolist()
    proj = h @ w_in
    g_q, g_k, g_v, g_lr, g_z, m_cq, m_ckv, m_kr, m_z, s_u, s_z = jnp.split(proj, points, axis=-1)

    q = g_q.reshape(bsz, t, GLA_HEADS, GLA_DK) * (GLA_DK ** -0.5)
    k = g_k.reshape(bsz, t, GLA_HEADS, GLA_DK)
    v = g_v.reshape(bsz, t, GLA_HEADS, GLA_DV)
    gate_logit = (g_lr @ gla_w_gate + gla_b_gate).astype(f32)
    log_a = (jax.nn.log_sigmoid(gate_logit) / GLA_GATE_TAU).reshape(bsz, t, GLA_HEADS, GLA_DK)
    o_gla, gla_s = gla_recurrence(q, k, v, log_a, gla_s0)
    o_gla = rmsnorm(o_gla, gla_norm_gain).reshape(bsz, t, GLA_WIDTH).astype(h.dtype) * jax.nn.silu(g_z)

    c_q = rmsnorm(m_cq, mla_q_norm_gain)
    qh = (c_q @ mla_w_uq).reshape(bsz, t, MLA_HEADS, MLA_NOPE_DIM + MLA_ROPE_DIM)
    q_nope = qh[..., :MLA_NOPE_DIM]
    q_pe = apply_rope(qh[..., MLA_NOPE_DIM:], q_pos)
    ckv_new = rmsnorm(m_ckv, mla_kv_norm_gain)
    kpe_new = apply_rope(m_kr, q_pos)
    ckv_all = jnp.concatenate([ckv_past.astype(h.dtype), ckv_new], axis=1)
    kpe_all = jnp.concatenate([kpe_past.astype(h.dtype), kpe_new], axis=1)
    kv = (ckv_all @ mla_w_ukv).reshape(bsz, past + t, MLA_HEADS, MLA_NOPE_DIM + MLA_V_DIM)
    k_nope, v_mla = kv[..., :MLA_NOPE_DIM], kv[..., MLA_NOPE_DIM:]
    o_mla = chunk_causal_attention(q_nope, q_pe, q_pos, k_nope, kpe_all, v_mla, k_pos)
    o_mla = o_mla.reshape(bsz, t, MLA_WIDTH).astype(h.dtype) * jax.nn.silu(m_z)

    y5, s_re, s_im = s5_scan(s_u, s5_lambda_re, s5_lambda_im, s5_b_re, s5_b_im, s5_c_re, s5_c_im,
                             s5_d, s5_log_dt, s5_x0_re, s5_x0_im)
    g5 = jax.nn.gelu(y5)
    y5 = g5 * jax.nn.sigmoid(g5 @ s5_w_glu.astype(f32) + s5_b_glu.astype(f32))
    o_s5 = y5.astype(h.dtype) * jax.nn.silu(s_z)

    out = jnp.concatenate([o_gla, o_mla, o_s5], axis=-1) @ w_out
    return out, gla_s, ckv_new, kpe_new, s_re, s_im


def run_trunk(x, gla_state, ckv_cache, kpe_cache, s5_re, s5_im, ln_gain, final_gain, layer_weights):
    gla_o, ckv_o, kpe_o, re_o, im_o = [], [], [], [], []
    for l in range(DEPTH):
        lw = [w[l] for w in layer_weights]
        mix, g_s, c_n, k_n, r_s, i_s = hybrid_layer(rmsnorm(x, ln_gain[l]), gla_state[l], ckv_cache[l],
                                                    kpe_cache[l], s5_re[l], s5_im[l], *lw)
        x = x + mix.astype(x.dtype)
        gla_o.append(g_s)
        ckv_o.append(c_n)
        kpe_o.append(k_n)
        re_o.append(r_s)
        im_o.append(i_s)
    y = rmsnorm(x, final_gain)
    return y, jnp.stack(gla_o), jnp.stack(ckv_o), jnp.stack(kpe_o), jnp.stack(re_o), jnp.stack(im_o)


def setup_inputs(seed: int = 0) -> dict:
    key = jax.random.key(seed)
    ks = jax.random.split(key, 28)
    f32 = jnp.float32
    nrm = lambda k, shape, s=1.0: jax.random.normal(k, shape, f32) * s
    return {
        'x_prompt': nrm(ks[0], (BATCH, SEQ, D_MODEL)),
        'x_sample': nrm(ks[1], (DEC_BATCH, DEC_SEQ, D_MODEL)),
        'state_gla': nrm(ks[2], (DEPTH, DEC_BATCH, GLA_HEADS, GLA_DK, GLA_DV), 0.3),
        'cache_mla_ckv': nrm(ks[3], (DEPTH, DEC_BATCH, PAST_LEN, MLA_KV_RANK)),
        'cache_mla_kpe': nrm(ks[4], (DEPTH, DEC_BATCH, PAST_LEN, MLA_ROPE_DIM)),
        'state_s5_re': nrm(ks[5], (DEPTH, DEC_BATCH, S5_GROUPS, S5_STATE), 0.1),
        'state_s5_im': nrm(ks[6], (DEPTH, DEC_BATCH, S5_GROUPS, S5_STATE), 0.1),
        'ln_gain': 1.0 + nrm(ks[7], (DEPTH, D_MODEL), 0.02),
        'w_in': nrm(ks[8], (DEPTH, D_MODEL, IN_COLS), D_MODEL ** -0.5),
        'gla_w_gate': nrm(ks[9], (DEPTH, GLA_GATE_RANK, GLA_HEADS * GLA_DK), GLA_GATE_RANK ** -0.5),
        'gla_b_gate': nrm(ks[10], (DEPTH, GLA_HEADS * GLA_DK), 0.1),
        'gla_norm_gain': 1.0 + nrm(ks[11], (DEPTH, GLA_DV), 0.02),
        'mla_q_norm_gain': 1.0 + nrm(ks[12], (DEPTH, MLA_Q_RANK), 0.02),
        'mla_w_uq': nrm(ks[13], (DEPTH, MLA_Q_RANK, MLA_HEADS * (MLA_NOPE_DIM + MLA_ROPE_DIM)), MLA_Q_RANK ** -0.5),
        'mla_kv_norm_gain': 1.0 + nrm(ks[14], (DEPTH, MLA_KV_RANK), 0.02),
        'mla_w_ukv': nrm(ks[15], (DEPTH, MLA_KV_RANK, MLA_HEADS * (MLA_NOPE_DIM + MLA_V_DIM)), MLA_KV_RANK ** -0.5),
        's5_lambda_re': -0.5 + nrm(ks[16], (DEPTH, S5_GROUPS, S5_STATE), 0.01),
        's5_lambda_im': jnp.pi * jnp.arange(S5_STATE, dtype=f32)[None, None, :] + nrm(ks[17], (DEPTH, S5_GROUPS, S5_STATE), 0.01),
        's5_b_re': nrm(ks[18], (DEPTH, S5_GROUPS, S5_STATE, S5_GROUP_CH), (2 * S5_GROUP_CH) ** -0.5),
        's5_b_im': nrm(ks[19], (DEPTH, S5_GROUPS, S5_STATE, S5_GROUP_CH), (2 * S5_GROUP_CH) ** -0.5),
        's5_c_re': nrm(ks[20], (DEPTH, S5_GROUPS, S5_GROUP_CH, S5_STATE), (2 * S5_STATE) ** -0.5),
        's5_c_im': nrm(ks[21], (DEPTH, S5_GROUPS, S5_GROUP_CH, S5_STATE), (2 * S5_STATE) ** -0.5),
        's5_d': nrm(ks[22], (DEPTH, S5_WIDTH)),
        's5_log_dt': jax.random.uniform(ks[23], (DEPTH, S5_GROUPS), f32, math.log(DT_MIN), math.log(DT_MAX)),
        's5_w_glu': nrm(ks[24], (DEPTH, S5_WIDTH, S5_WIDTH), S5_WIDTH ** -0.5),
        's5_b_glu': nrm(ks[25], (DEPTH, S5_WIDTH), 0.02),
        'w_out': nrm(ks[26], (DEPTH, D_MIX, D_MODEL), D_MIX ** -0.5),
        'final_gain': 1.0 + nrm(ks[27], (D_MODEL,), 0.02),
    }


def reference(x_prompt, x_sample, state_gla, cache_mla_ckv, cache_mla_kpe, state_s5_re, state_s5_im,
              ln_gain, w_in, gla_w_gate, gla_b_gate, gla_norm_gain,
              mla_q_norm_gain, mla_w_uq, mla_kv_norm_gain, mla_w_ukv,
              s5_lambda_re, s5_lambda_im, s5_b_re, s5_b_im, s5_c_re, s5_c_im,
              s5_d, s5_log_dt, s5_w_glu, s5_b_glu, w_out, final_gain):
    f32 = jnp.float32
    layer_weights = (w_in, gla_w_gate, gla_b_gate, gla_norm_gain,
                     mla_q_norm_gain, mla_w_uq, mla_kv_norm_gain, mla_w_ukv,
                     s5_lambda_re, s5_lambda_im, s5_b_re, s5_b_im, s5_c_re, s5_c_im,
                     s5_d, s5_log_dt, s5_w_glu, s5_b_glu, w_out)
    bp = x_prompt.shape[0]
    gla0 = jnp.zeros((DEPTH, bp, GLA_HEADS, GLA_DK, GLA_DV), f32)
    ckv0 = jnp.zeros((DEPTH, bp, 0, MLA_KV_RANK), x_prompt.dtype)
    kpe0 = jnp.zeros((DEPTH, bp, 0, MLA_ROPE_DIM), x_prompt.dtype)
    s50 = jnp.zeros((DEPTH, bp, S5_GROUPS, S5_STATE), f32)
    y_prompt, gla_p, ckv_p, kpe_p, s5re_p, s5im_p = run_trunk(
        x_prompt, gla0, ckv0, kpe0, s50, s50, ln_gain, final_gain, layer_weights)
    y_sample, gla_s, ckv_s, kpe_s, s5re_s, s5im_s = run_trunk(
        x_sample, state_gla, cache_mla_ckv, cache_mla_kpe, state_s5_re, state_s5_im,
        ln_gain, final_gain, layer_weights)
    return (y_prompt, y_sample, gla_p, ckv_p, kpe_p, s5re_p, s5im_p, gla_s, ckv_s, kpe_s, s5re_s, s5im_s)
```

```python
import math
import numpy as np
from contextlib import ExitStack
import concourse.bass as bass
import concourse.mybir as mybir
from concourse.bass_utils import run_bass_kernel_spmd

F32 = mybir.dt.float32
BF16 = mybir.dt.bfloat16
ALU = mybir.AluOpType
AF = mybir.ActivationFunctionType

N_CORES = 8
D = 1024
SEQ = 2048
DEC_SEQ = 16
SPC = 4
NS = SPC * DEC_SEQ
NTOK = SEQ + NS
PAST = 1024
NB = 256
NPB = SEQ // NB
NT_IN = 18
WIN_COLS = NT_IN * 128
VL = 39
NV = 2 * VL + 8
EPS = 1e-6
MAGIC = 12582912.0
GELU_C = math.sqrt(2.0 / math.pi)

C_ID, C_ONES, C_BLK, C_CM64, C_CM16, C_GM64, C_GM16 = 0, 128, 256, 384, 448, 464, 720
C_HM = 784
NCONST = 788


class _Op:
    __slots__ = ("eng", "fn", "deps", "dma", "chan", "signal", "sigval", "chanval", "cost", "lat", "tag")

    def __init__(self, eng, fn, dma, chan, cost, lat):
        self.eng = eng; self.fn = fn; self.deps = {}; self.dma = dma; self.chan = chan
        self.signal = False; self.sigval = 0; self.chanval = 0
        self.cost = cost
        self.lat = lat


class Sched:
    def __init__(self, nc, es):
        self.nc = nc
        self.es = es
        self.ops = []
        self.last_w = {}
        self.readers = {}
        self.engobj = {"pe": nc.tensor, "act": nc.scalar, "dve": nc.vector, "pool": nc.gpsimd, "sp": nc.sync}
        self.chan_last = {}

    def sb(self, name, shape, dtype=F32):
        return self.es.enter_context(self.nc.sbuf_tensor("s_" + name, list(shape), dtype))

    def ps(self, name, shape, dtype=F32):
        return self.es.enter_context(self.nc.psum_tensor(name, list(shape), dtype))

    def op(self, eng, fn, r=(), w=(), dma=False, chan=None, cost=300.0, lat=0.0):
        idx = len(self.ops)
        o = _Op(eng, fn, dma, chan, cost, lat)
        o.tag = ""
        if getattr(self, "want_tags", False):
            import sys as _sys
            f = _sys._getframe(1)
            for _ in range(8):
                if f is None:
                    break
                nm = f.f_code.co_name
                if nm in ("proj_and_norm", "sumsq_rstd", "gla_block", "mla_qkv", "mla_prompt_pre", "mla_prompt_gen", "attn_tail_one", "mla_sample",
                          "s5_pre", "s5_chunks_gen", "s5_tail", "out_block", "load_weights", "s5_setup"):
                    o.tag = nm
                    break
                f = f.f_back
        dres = self.__dict__.setdefault("dres", {})
        for res in r:
            j = self.last_w.get(res)
            if j is not None:
                o.deps[j] = "RAW"; dres[(idx, j)] = res
        for res in w:
            j = self.last_w.get(res)
            if j is not None:
                o.deps[j] = "RAW" if o.deps.get(j) == "RAW" else "WAW"; dres.setdefault((idx, j), res)
            for j in self.readers.get(res, ()):
                if j != idx and j not in o.deps:
                    o.deps[j] = "WAR"; dres[(idx, j)] = res
        if dma:
            j = self.chan_last.get(chan)
            if j is not None and j not in o.deps:
                o.deps[j] = "CHAN"
            self.chan_last[chan] = idx
        for res in r:
            self.readers.setdefault(res, []).append(idx)
        for res in w:
            self.last_w[res] = idx
            self.readers[res] = []
        self.ops.append(o)
        return idx

    def dma(self, out, in_, r=(), w=(), chan=None, queue="sp", **kw):
        eng = self.engobj[queue]
        try:
            nbytes = float(out.nbytes())
        except Exception:
            nbytes = 65536.0
        return self.op(queue, lambda: eng.dma_start(out=out, in_=in_, **kw), r=r, w=w, dma=True, chan=chan,
                       cost=(120.0 if queue == "sp" else 1000.0), lat=2500.0 + nbytes / 150.0)

    def schedule(self):
        import heapq
        ops = self.ops
        n = len(ops)
        succ = [[] for _ in range(n)]
        ndep = [0] * n
        for i, o in enumerate(ops):
            ndep[i] = len(o.deps)
            for j in o.deps:
                succ[j].append(i)
        engs = ["pe", "act", "dve", "pool", "sp"]
        free = {e: 0.0 for e in engs}
        fut = {e: [] for e in engs}
        avail = {e: [] for e in engs}
        ready_t = [0.0] * n
        fin = [0.0] * n
        start = [0.0] * n
        for i, o in enumerate(ops):
            if ndep[i] == 0:
                heapq.heappush(fut[o.eng], (0.0, i))
        done = 0
        order = []
        SYNC = 120.0
        binder = {}
        blame = {}
        while done < n:
            best = None
            for e in engs:
                f = fut[e]; a = avail[e]
                while f and f[0][0] <= free[e]:
                    heapq.heappush(a, heapq.heappop(f)[1])
                if a:
                    cand = (free[e], a[0], e, True)
                elif f:
                    cand = (f[0][0], f[0][1], e, False)
                else:
                    continue
                if best is None or cand[:2] < best[:2]:
                    best = cand
            st, i, e, from_avail = best
            if from_avail:
                heapq.heappop(avail[e])
            else:
                heapq.heappop(fut[e])
            o = ops[i]
            if st > free[e] + 1.0 and i in binder:
                jb = binder[i]
                key = (getattr(self, "dres", {}).get((i, jb), "?"), o.deps.get(jb, "?"), o.eng)
                blame[key] = blame.get(key, 0.0) + (st - free[e])
            start[i] = st
            free[e] = st + o.cost
            fin[i] = st + o.cost + o.lat
            order.append(i)
            done += 1
            for k in succ[i]:
                rt = fin[i] + (((200.0 if e == "pe" else SYNC)) if ops[k].eng != e or o.dma else (20.0 if e == "pe" else 110.0))
                if rt > ready_t[k]:
                    ready_t[k] = rt
                    binder[k] = i
                ndep[k] -= 1
                if ndep[k] == 0:
                    heapq.heappush(fut[ops[k].eng], (ready_t[k], k))
        pos = {old: new for new, old in enumerate(order)}
        new_ops = [ops[i] for i in order]
        for o in new_ops:
            o.deps = {pos[j]: kind for j, kind in o.deps.items()}
        self.ops = new_ops
        self.blame = blame
        self.sim_start = {id(ops[i]): start[i] for i in range(n)}
        self.sim_binder = {id(ops[k]): ops[j] for k, j in binder.items()}
        self.sim_makespan = max(fin) if fin else 0.0
        return self.sim_makespan

    def emit(self):
        nc = self.nc
        ops = self.ops
        engs = ["pe", "act", "dve", "pool", "sp"]
        need = []
        for i, o in enumerate(ops):
            lst = []
            best = {}
            for j, kind in o.deps.items():
                pj = ops[j]
                if pj.dma:
                    lst.append(j)
                    continue
                if pj.eng == o.eng and not o.dma:
                    if o.eng == "pe":
                        continue
                if pj.eng not in best or j > best[pj.eng]:
                    best[pj.eng] = j
            for e, j in best.items():
                lst.append(j)
                ops[j].signal = True
            need.append(lst)
        last_on = {}
        for i, o in enumerate(ops):
            if not o.dma:
                last_on[o.eng] = i
        for e, i in last_on.items():
            ops[i].signal = True
        sem = {e: self.es.enter_context(nc.semaphore("sem_" + e)) for e in engs}
        chans = sorted({o.chan for o in ops if o.dma})
        csem = {c: self.es.enter_context(nc.semaphore("dch_" + str(c))) for c in chans}
        cnt = {e: 0 for e in engs}
        ccnt = {c: 0 for c in chans}
        known = {e: {} for e in engs}
        nwaits = 0
        for i, o in enumerate(ops):
            eo = self.engobj[o.eng]
            kn = known[o.eng]
            for j in need[i]:
                pj = ops[j]
                if pj.dma:
                    key = ("c", pj.chan); val = pj.chanval; s = csem[pj.chan]
                else:
                    key = ("e", pj.eng); val = pj.sigval; s = sem[pj.eng]
                if kn.get(key, 0) >= val:
                    continue
                eo.wait_ge(s, val)
                nwaits += 1
                kn[key] = val
            ins = o.fn()
            if o.dma:
                ccnt[o.chan] += 16
                o.chanval = ccnt[o.chan]
                ins.then_inc(csem[o.chan], 16)
            elif o.signal:
                cnt[o.eng] += 1
                o.sigval = cnt[o.eng]
                ins.then_inc(sem[o.eng], 1)
        for c in chans:
            if ccnt[c] > 0 and known["sp"].get(("c", c), 0) < ccnt[c]:
                nc.sync.wait_ge(csem[c], ccnt[c])
        for e in ("pe", "act", "dve", "pool"):
            if cnt[e] > 0:
                nc.sync.wait_ge(sem[e], cnt[e])
        fin = self.es.enter_context(nc.semaphore("sem_fin"))
        for e in ("pe", "act", "dve", "pool"):
            self.engobj[e].nop().then_inc(fin, 1)
        nc.sync.wait_ge(fin, 4)
        return dict(n_ops=len(ops), n_waits=nwaits, signals=dict(cnt), n_chans=len(chans))


def build_program(cfg=None):
    cfg = cfg or {}
    c_layers = cfg.get('layers', 2); c_blocks = cfg.get('blocks', list(range(NPB + 1))); c_st = cfg.get('stages', ('proj', 'gla', 'qkv', 'mla', 's5', 'out')); c_setup = cfg.get('setup', True)
    nc = bass.Bass("TRN2", target_bir_lowering=False)
    es = ExitStack()
    S = Sched(nc, es)
    S.want_tags = bool(cfg.get('tags'))
    PE, ACT, DVE, POOL = "pe", "act", "dve", "pool"
    T, V, A, G = nc.tensor, nc.vector, nc.scalar, nc.gpsimd

    def din(name, shape):
        return nc.dram_tensor(name, list(shape), F32, kind="ExternalInput").ap()

    def dout(name, shape):
        return nc.dram_tensor(name, list(shape), F32, kind="ExternalOutput").ap()

    d_xT = din("xT", [128, 8, NTOK])
    d_win = din("win", [2, 128, 8, WIN_COLS])
    d_wout = din("wout", [2, 128, 8, D])
    d_wuq = din("wuq", [2, 128, 2, 768])
    d_wukv = din("wukv", [2, 128, 768])
    d_wgate = din("wgate", [2, 16, 128])
    d_wglu = din("wglu", [2, 128, 2, 256])
    d_wB = din("wB", [2, 2, 128, 8, 128])
    d_wC = din("wC", [2, 2, 128, 8, 128])
    d_dD = din("dD", [2, 128, 2, 128])
    d_vecs = din("vecs", [128, NV])
    d_consts = din("consts", [128, NCONST])
    d_rope = din("rope", [2, 128, NTOK])
    d_gla0 = din("gla0", [2, SPC, 128, 64])
    d_s5x0 = din("s5x0", [2, 2, SPC, 128, 8])
    d_ckvP = din("ckvP", [2, SPC, 128, PAST])
    d_kpeP = din("kpeP", [2, SPC, 32, PAST])
    o_yT = dout("yT", [128, 8, NTOK])
    o_gla = dout("glaO", [2, 1 + SPC, 128, 64])
    o_ckv = dout("ckvO", [2, 128, NTOK])
    o_kpe = dout("kpeO", [2, 32, NTOK])
    o_s5 = dout("s5O", [2, 2, 1 + SPC, 128, 8])
    d_x1 = nc.dram_tensor("x1scr", [128, 8, NTOK], F32).ap()

    sb = S.sb
    consts = sb("consts", [128, NCONST])
    cbf = sb("cbf", [128, 384], BF16)
    vecs = sb("vecs", [128, NV])
    nvec = sb("nvec", [128, 8])
    win_bf = sb("win_bf", [128, 8, WIN_COLS], BF16)
    wout_bf = sb("wout_bf", [128, 8, D], BF16)
    wuq_bf = sb("wuq_bf", [128, 2, 768], BF16)
    wukv_bf = sb("wukv_bf", [128, 768], BF16)
    wgate_bf = sb("wgate_bf", [16, 128], BF16)
    wglu_bf = sb("wglu_bf", [128, 2, 256], BF16)
    wB_bf = sb("wB_bf", [128, 2, 8, 128], BF16)
    wC_bf = sb("wC_bf", [128, 2, 8, 128], BF16)
    dD_bf = sb("dD_bf", [128, 2, 128], BF16)

    s5sm = sb("s5sm", [128, 16, 8])
    SL = 64
    cosT = sb("cosT", [128, 8, SL]); sinT = sb("sinT", [128, 8, SL])
    T1r = sb("T1r", [128, 8, SL]); T1i = sb("T1i", [128, 8, SL]); Rm = sb("Rm", [128, 8, SL])
    RmS = sb("RmS", [128, 8, SPC, DEC_SEQ])
    xpr = sb("xpr", [128, SPC, 8]); xpi = sb("xpi", [128, SPC, 8])
    rxr = sb("rxr", [128, 8, SPC]); rxi = sb("rxi", [128, 8, SPC])
    xr_bf = sb("xr_bf", [128, 8, SL], BF16); xi_bf = sb("xi_bf", [128, 8, SL], BF16)
    xblks = [sb("xblk%d" % i, [128, 8, NB]) for i in range(2)]
    cur = {"x": xblks[0], "n": "xblk0"}
    sqr = [sb("sqr%d" % i, [128, NB], BF16) for i in range(4)]
    h_bf = sb("h_bf", [128, 8, NB], BF16)
    rstd = sb("rstd", [128, NB])
    pj = sb("pj", [128, NT_IN, NB])
    s5t = [pj[:, 2 * i:2 * i + 2, :].rearrange("p a n -> p (a n)") for i in range(6)]
    s5tn = [["pj%d" % (2 * i), "pj%d" % (2 * i + 1)] for i in range(6)]
    tmpAll = sb("tmpAll", [128, 8, NB])
    tmp = [tmpAll[:, i, :] for i in range(8)]
    _pp = [(0, 1), (2, 3), (4, 5), (6, 7), (8, 9), (14, 15)]
    sx = [pj[:, a:a + 2, :].rearrange("p a n -> p (a n)") for (a, _) in _pp] + \
         [tmpAll[:, 0:2, :].rearrange("p a n -> p (a n)"), tmpAll[:, 2:4, :].rearrange("p a n -> p (a n)")]
    sxn = [["pj%d" % a, "pj%d" % b_] for (a, b_) in _pp] + [["tmp0", "tmp1"], ["tmp2", "tmp3"]]
    tb = [sb("tb%d" % i, [128, 2, NB], BF16) for i in range(2)]
    glr_bf = sb("glr_bf", [16, NB], BF16)
    qt_bf = sb("qt_bf", [128, NB], BF16); kt_bf = sb("kt_bf", [128, NB], BF16)
    dec = sb("dec", [128, 4])
    vtok_bf = sb("vtok_bf", [64, 4, 256], BF16); ktok_bf = sb("ktok_bf", [64, 4, 128], BF16)
    scT_bf = sb("scT_bf", [64, 4, 4, 64], BF16)
    Sst = sb("Sst", [128, 64]); Stmp = sb("Stmp", [128, 64]); Sin = sb("Sin", [128, 4, 64]); Sall_bf = sb("Sall_bf", [128, 4, 4, 64], BF16)
    ktx_bf = sb("ktx_bf", [128, 4, NB], BF16)
    ocat_bf = sb("ocat_bf", [128, 8, NB], BF16)
    cqn_bf = sb("cqn_bf", [128, 2, NB], BF16)
    Q_bf = sb("Q_bf", [128, 4, NB], BF16)
    ropec = sb("ropec", [128, NB]); ropes = sb("ropes", [128, NB])
    ckv_f = sb("ckv_f", [128, NB]); ckv_bf = sb("ckv_bf", [128, NB], BF16)
    kpe_f = sb("kpe_f", [128, NB])
    Kbuf = sb("Kbuf", [128, 4, SEQ], BF16)
    Vbuf = sb("Vbuf", [128, 16, 512], BF16)
    pT = [sb("pT%d" % i, [128, NB], BF16) for i in range(3)]
    ckvall_bf = sb("ckvall_bf", [128, PAST + DEC_SEQ], BF16)
    u_bf = sb("u_bf", [128, 2, NB], BF16)
    y_sb = ckvall_bf[:, 0:1024].bitcast(F32).rearrange("p (a n) -> p a n", a=2)
    g5 = sb("g5", [128, 2, NB]); g5_bf = sb("g5_bf", [128, 2, NB], BF16)
    big = [sb("big%d" % i, [128, 4, NB]) for i in range(2)]
    ybuf = pj

    banks = [S.ps("bank%d" % i, [128, 512]) for i in range(8)]
    pool_i = [0]

    def pbank():
        npool = cfg.get('wi_pool', 3)
        i = pool_i[0] % npool
        pool_i[0] += 1
        if i >= 3:
            return banks[i % 2], "wibank%d" % i
        bi_ = (0, 1, 7)[i]
        return banks[bi_], "bank%d" % bi_

    pool2_i = [0]

    def pbank2():
        i = (2, 3, 6)[pool2_i[0] % 3]
        pool2_i[0] += 1
        return banks[i], ("bank%d" % i if i != 6 else "bank6a")

    BIG0 = ["big0a", "big0b"]; BIG1 = ["big1a", "big1b"]

    def fsz(ap):
        n = 1
        for d in ap.shape[1:]:
            n *= int(d)
        return float(n)

    def ecost(eng, ap, per=None):
        n = fsz(ap)
        if eng == ACT:
            return 190.0 + 0.70 * n
        if eng == DVE:
            return 150.0 + 0.86 * (per if per is not None else 1.05) * n
        return 280.0 + (per if per is not None else 2.2) * n

    def act(out, in_, func, r, w, bias=0.0, scale=1.0):
        S.op(ACT, lambda: A.activation(out=out, in_=in_, func=func, bias=bias, scale=scale), r=r, w=w, cost=ecost(ACT, out))

    P2D = cfg.get('pool_to_dve', True)

    def tt(eng, out, in0, in1, op, r, w):
        if P2D and eng == POOL:
            eng = DVE
        e = V if eng == DVE else G
        S.op(eng, lambda: e.tensor_tensor(out=out, in0=in0, in1=in1, op=op), r=r, w=w, cost=ecost(eng, out))

    def ts(eng, out, in0, s1, s2, op0, op1, r, w):
        if P2D and eng == POOL:
            eng = DVE
        e = V if eng == DVE else G
        if s2 is None:
            S.op(eng, lambda: e.tensor_scalar(out=out, in0=in0, scalar1=s1, scalar2=None, op0=op0), r=r, w=w, cost=ecost(eng, out, 0.6))
        else:
            S.op(eng, lambda: e.tensor_scalar(out=out, in0=in0, scalar1=s1, scalar2=s2, op0=op0, op1=op1), r=r, w=w, cost=ecost(eng, out, 0.6))

    def stt(eng, out, in0, scalar, in1, op0, op1, r, w):
        e = V if eng == DVE else G
        S.op(eng, lambda: e.scalar_tensor_tensor(out=out, in0=in0, scalar=scalar, in1=in1, op0=op0, op1=op1), r=r, w=w, cost=ecost(eng, out))

    def cp(eng, out, in_, r, w):
        if P2D and eng == POOL:
            eng = DVE
        if eng == ACT:
            S.op(ACT, lambda: A.copy(out=out, in_=in_), r=r, w=w, cost=ecost(ACT, out))
        else:
            e = V if eng == DVE else G
            S.op(eng, lambda: e.tensor_copy(out=out, in_=in_), r=r, w=w, cost=ecost(eng, out, 0.8))

    def mm(out, lhsT, rhs, start, stop, r, w, tp=None):
        c = 35.0 + 0.58 * max(64.0, fsz(out))
        if tp is None:
            S.op(PE, lambda: T.matmul(out, lhsT=lhsT, rhs=rhs, start=start, stop=stop), r=r, w=w, cost=c)
        else:
            S.op(PE, lambda: T.matmul(out, lhsT=lhsT, rhs=rhs, start=start, stop=stop, tile_position=tp), r=r, w=w, cost=c)

    def rsqrt_chain(out, ps_in, scale, r, w, tmpap, tmpname):
        act(tmpap, ps_in, AF.Ln, r=r, w=[tmpname], bias=EPS, scale=scale)
        act(out, tmpap, AF.Exp, r=[tmpname], w=w, scale=-0.5)

    def sigmoid_chain(out, in_, r, w, t1, t1n, scale=1.0, nbias=0.0):
        t1n = [t1n] if isinstance(t1n, str) else list(t1n)
        act(t1, in_, AF.Exp, r=r, w=t1n, bias=nbias, scale=-scale)
        act(t1, t1, AF.Ln, r=t1n, w=t1n, bias=1.0)
        act(out, t1, AF.Exp, r=t1n, w=w, scale=-1.0)

    ident_bf = cbf[:, 0:128]
    ones_bf = cbf[:, 128:256]
    blk_bf = cbf[:, 256:384]

    S.dma(consts[:], d_consts[:, :], w=["consts"], chan="ld0")
    S.dma(vecs[:], d_vecs[:, :], w=["vecs"], chan="ld1")
    cp(DVE, cbf[:], consts[:, 0:384], r=["consts"], w=["cbf"])
    for l in range(2):
        b = l * VL
        ts(DVE, nvec[:, 4 * l:4 * l + 1], vecs[:, b + 11:b + 12], -1.0, None, ALU.mult, None, r=["vecs"], w=["nvec"])
        ts(DVE, nvec[:, 4 * l + 1:4 * l + 3], vecs[:, b + 13:b + 15], -1.0, None, ALU.mult, None, r=["vecs"], w=["nvec"])

    wl_i = [0]

    def wdma(dst, src_ap, wname):
        c = "wl%d" % (wl_i[0] % 6)
        wl_i[0] += 1
        S.dma(dst, src_ap, w=[wname], chan=c, queue="pool")

    def load_weights(l):
        for kt in range(8):
            wdma(win_bf[:, kt, :], d_win[l, :, kt, :], "win_bf%d" % kt)
        for kt in range(8):
            wdma(wout_bf[:, kt, :], d_wout[l, :, kt, :], "wout_bf%d" % kt)
        for a in range(2):
            wdma(wuq_bf[:, a, :], d_wuq[l, :, a, :], "wuq_bf")
        wdma(wukv_bf[:], d_wukv[l], "wukv_bf")
        wdma(wglu_bf[:].rearrange("p a c -> p (a c)"), d_wglu[l].rearrange("p a c -> p (a c)"), "wglu_bf")
        wdma(dD_bf[:].rearrange("p a c -> p (a c)"), d_dD[l].rearrange("p a c -> p (a c)"), "dD_bf")
        wdma(wgate_bf[:], d_wgate[l], "wgate_bf")
        for ri in range(2):
            wdma(wB_bf[:, ri].rearrange("p a c -> p (a c)"), d_wB[l, ri].rearrange("p a c -> p (a c)"), "wB_bf")
            wdma(wC_bf[:, ri].rearrange("p a c -> p (a c)"), d_wC[l, ri].rearrange("p a c -> p (a c)"), "wC_bf")

    def s5_setup(l):
        b = l * VL
        lam_re = vecs[:, b + 15:b + 23]; lam_im = vecs[:, b + 23:b + 31]; log_dt = vecs[:, b + 31:b + 39]
        sm = lambda i: s5sm[:, i, :]
        R = ["vecs", "s5sm"]; W = ["s5sm"]
        act(sm(0), log_dt, AF.Exp, r=R, w=W)
        tt(DVE, sm(1), lam_re, sm(0), ALU.mult, r=R, w=W)
        tt(DVE, sm(2), lam_im, sm(0), ALU.mult, r=R, w=W)
        act(sm(3), sm(1), AF.Exp, r=R, w=W)
        for (dst, off) in ((4, 0.0), (5, math.pi / 2)):
            ts(DVE, sm(6), sm(2), off, None, ALU.add, None, r=R, w=W)
            ts(DVE, sm(7), sm(6), float(1 / (2 * math.pi)), MAGIC, ALU.mult, ALU.add, r=R, w=W)
            ts(DVE, sm(7), sm(7), MAGIC, float(-2 * math.pi), ALU.subtract, ALU.mult, r=R, w=W)
            tt(DVE, sm(6), sm(6), sm(7), ALU.add, r=R, w=W)
            act(sm(dst), sm(6), AF.Sin, r=R, w=W)
        tt(DVE, sm(6), sm(3), sm(5), ALU.mult, r=R, w=W)
        tt(DVE, sm(7), sm(3), sm(4), ALU.mult, r=R, w=W)
        ts(DVE, sm(8), sm(6), -1.0, None, ALU.add, None, r=R, w=W)
        tt(DVE, sm(9), sm(8), lam_re, ALU.mult, r=R, w=W)
        tt(DVE, sm(10), sm(7), lam_im, ALU.mult, r=R, w=W)
        tt(DVE, sm(9), sm(9), sm(10), ALU.add, r=R, w=W)
        tt(DVE, sm(10), sm(7), lam_re, ALU.mult, r=R, w=W)
        tt(DVE, sm(11), sm(8), lam_im, ALU.mult, r=R, w=W)
        tt(DVE, sm(10), sm(10), sm(11), ALU.subtract, r=R, w=W)
        tt(DVE, sm(11), lam_re, lam_re, ALU.mult, r=R, w=W)
        tt(DVE, sm(12), lam_im, lam_im, ALU.mult, r=R, w=W)
        tt(DVE, sm(11), sm(11), sm(12), ALU.add, r=R, w=W)
        S.op(DVE, lambda: V.reciprocal(out=sm(11), in_=sm(11)), r=R, w=W)
        tt(DVE, sm(9), sm(9), sm(11), ALU.mult, r=R, w=W)
        tt(DVE, sm(10), sm(10), sm(11), ALU.mult, r=R, w=W)
        t0n, t1n, t2n, t3n = s5tn[0], s5tn[1], s5tn[2], s5tn[3]
        RT = ["s5sm", "cosT", "sinT"] + t0n + t1n; WT = ["cosT", "sinT"] + t0n + t1n
        cp(DVE, cosT[:, :, 0:1], s5sm[:, 5, :].unsqueeze(2), r=RT, w=WT)
        cp(DVE, sinT[:, :, 0:1], s5sm[:, 4, :].unsqueeze(2), r=RT, w=WT)
        m = 1
        while m < SL:
            br = cosT[:, :, m - 1:m].to_broadcast([128, 8, m]); bi = sinT[:, :, m - 1:m].to_broadcast([128, 8, m])
            a_ = s5t[0][:, 0:8 * m].rearrange("p (s t) -> p s t", s=8); b_ = s5t[1][:, 0:8 * m].rearrange("p (s t) -> p s t", s=8)
            tt(DVE, a_, cosT[:, :, 0:m], br, ALU.mult, r=RT, w=WT)
            tt(DVE, b_, sinT[:, :, 0:m], bi, ALU.mult, r=RT, w=WT)
            tt(DVE, cosT[:, :, m:2 * m], a_, b_, ALU.subtract, r=RT, w=WT)
            tt(DVE, a_, cosT[:, :, 0:m], bi, ALU.mult, r=RT, w=WT)
            tt(DVE, b_, sinT[:, :, 0:m], br, ALU.mult, r=RT, w=WT)
            tt(DVE, sinT[:, :, m:2 * m], a_, b_, ALU.add, r=RT, w=WT)
            m *= 2
        crb = s5sm[:, 9, :].unsqueeze(2).to_broadcast([128, 8, SL]); cib = s5sm[:, 10, :].unsqueeze(2).to_broadcast([128, 8, SL])
        v3 = lambda t: t[:, 0:8 * SL].rearrange("p (s t) -> p s t", s=8)
        RT2 = ["s5sm", "cosT", "sinT"]
        tt(DVE, v3(s5t[0]), cosT[:, :, :], crb, ALU.mult, r=RT2, w=t0n)
        tt(DVE, v3(s5t[1]), sinT[:, :, :], cib, ALU.mult, r=RT2, w=t1n)
        tt(DVE, T1r[:, :, :], v3(s5t[0]), v3(s5t[1]), ALU.add, r=t0n + t1n, w=["T1r"])
        tt(DVE, v3(s5t[2]), cosT[:, :, :], cib, ALU.mult, r=RT2, w=t2n)
        tt(DVE, v3(s5t[3]), sinT[:, :, :], crb, ALU.mult, r=RT2, w=t3n)
        tt(DVE, T1i[:, :, :], v3(s5t[2]), v3(s5t[3]), ALU.subtract, r=t2n + t3n, w=["T1i"])
        cp(DVE, Rm[:, :, :], s5sm[:, 3, :].unsqueeze(2).to_broadcast([128, 8, SL]), r=["s5sm"], w=["Rm"])
        S.op(DVE, lambda: V.memset(Rm[:, :, 0:1], 0.0), r=[], w=["Rm"])
        cp(DVE, RmS[:, :, :, :], Rm[:, :, 0:DEC_SEQ].unsqueeze(2).to_broadcast([128, 8, SPC, DEC_SEQ]), r=["Rm"], w=["RmS"])

    def sumsq_rstd(N, scale):
        bk, bn = pbank()
        for kt in range(8):
            sq = sqr[kt % 4]; sqn = "sqr%d" % (kt % 4)
            act(sq[:, 0:N], cur["x"][:, kt, 0:N], AF.Square, r=["%s_%d" % (cur["n"], kt)], w=[sqn])
            mm(bk[:, 0:N], ones_bf, sq[:, 0:N], kt == 0, kt == 7, r=["cbf", sqn], w=[bn])
        rsqrt_chain(rstd[:, 0:N], bk[:, 0:N], scale, r=[bn], w=["rstd"], tmpap=tmp[0][:, 0:N], tmpname="tmp0")

    def proj_and_norm(l, blk, N, tok0):
        src_ = d_xT if l == 0 else d_x1
        for kt in range(8):
            S.dma(cur["x"][:, kt, 0:N], src_[:, kt, tok0:tok0 + N], r=(["x1_%d_%d" % (blk, kt)] if l == 1 else []),
                  w=["%s_%d" % (cur["n"], kt)], chan="xin%d" % kt)
        S.dma(ropec[64:96, 0:N], d_rope[0, 64:96, tok0:tok0 + N], w=["ropec"], chan="rp0")
        S.dma(ropes[64:96, 0:N], d_rope[1, 64:96, tok0:tok0 + N], w=["ropes"], chan="rp1")
        lvl = cfg.get('proj_lvl', 9)
        if lvl < 2:
            return
        sumsq_rstd(N, 1.0 / D)
        if lvl < 3:
            return
        for kt in range(8):
            stt(DVE, h_bf[:, kt, 0:N], cur["x"][:, kt, 0:N], vecs[:, l * VL + kt:l * VL + kt + 1], rstd[:, 0:N], ALU.mult, ALU.mult,
                r=["%s_%d" % (cur["n"], kt), "rstd", "vecs"], w=["h_bf%d" % kt])
        if lvl < 4:
            return
        for t_ in range(NT_IN):
            if t_ in (2, 3):
                continue
            bk, bn = pbank()
            for kt in range(8):
                mm(bk[:, 0:N], win_bf[:, kt, t_ * 128:(t_ + 1) * 128], h_bf[:, kt, 0:N], kt == 0, kt == 7,
                   r=["win_bf%d" % kt, "h_bf%d" % kt], w=[bn])
            if t_ in (14, 15):
                cp(ACT, u_bf[:, t_ - 14, 0:N], bk[:, 0:N], r=[bn], w=["u_bf"])
            else:
                cp(ACT, pj[:, t_, 0:N], bk[:, 0:N], r=[bn], w=["pj%d" % t_])

    def gla_block(l, blk, N, tok0, L, sample):
        nch = N // L
        b = l * VL
        cp(DVE, glr_bf[:, 0:N], pj[0:16, 4, 0:N], r=["pj4"], w=["glr_bf"])
        bk, bn = pbank()
        mm(bk[:, 0:N], wgate_bf[:, :], glr_bf[:, 0:N], True, True, r=["wgate_bf", "glr_bf"], w=[bn])
        e_ = tmp[0][:, 0:N]; c_ = tmp[1][:, 0:N]; eb = tmp[2][:, 0:N]; enb = tmp[3][:, 0:N]
        act(e_, bk[:, 0:N], AF.Exp, r=[bn, "nvec"], w=["tmp0"], bias=nvec[:, 4 * l:4 * l + 1], scale=-1.0)
        act(e_, e_, AF.Ln, r=["tmp0"], w=["tmp0"], bias=1.0)
        gm = consts[:, C_GM64:C_GM64 + N] if not sample else consts[:, C_GM16:C_GM16 + N]
        S.op(DVE, lambda: V.tensor_tensor_scan(out=c_, data0=gm, data1=e_, initial=0.0, op0=ALU.mult, op1=ALU.add),
             r=["tmp0", "consts"], w=["tmp1"])
        yield
        act(eb, c_, AF.Exp, r=["tmp1"], w=["tmp2"], scale=-1.0 / 16)
        act(enb, c_, AF.Exp, r=["tmp1"], w=["tmp3"], scale=1.0 / 16)
        c3 = tmp[1][:, 0:N].rearrange("p (c l) -> p c l", l=L)
        act(dec[:, 0:nch], c3[:, :, L - 1], AF.Exp, r=["tmp1"], w=["dec"], scale=-1.0 / 16)
        stt(DVE, qt_bf[:, 0:N], pj[:, 0, 0:N], 32 ** -0.5, eb, ALU.mult, ALU.mult, r=["pj0", "tmp2"], w=["qt_bf"])
        tt(DVE, kt_bf[:, 0:N], pj[:, 1, 0:N], enb, ALU.mult, r=["pj1", "tmp3"], w=["kt_bf"])
        hm = consts[:, C_HM:C_HM + 4]
        tt(DVE, ktx_bf[:, :, 0:N], kt_bf[:, 0:N].unsqueeze(1).to_broadcast([128, 4, N]),
           hm.unsqueeze(2).to_broadcast([128, 4, N]), ALU.mult, r=["kt_bf", "consts"], w=["ktx_bf"])
        if cfg.get('gla_lvl', 9) < 2:
            return
        yield
        for c2 in range(nch // 2):
            bk, bn = pbank()
            for cc in range(2):
                c = 2 * c2 + cc
                for kt in range(8):
                    mm(bk[0:L, cc * 256:(cc + 1) * 256], h_bf[:, kt, c * L:(c + 1) * L], win_bf[:, kt, 256:512],
                       kt == 0, kt == 7, r=["win_bf%d" % kt, "h_bf%d" % kt], w=[bn])
            cp(ACT, vtok_bf[0:L, 2 * c2:2 * c2 + 2, :], bk[0:L, :].rearrange("p (a c) -> p a c", a=2), r=[bn], w=["vtok_bf"])
            yield
        bk, bn = pbank()
        bkb = bk[:].bitcast(BF16)
        for c in range(nch):
            S.op(PE, lambda c=c: T.transpose(bkb[0:L, c * 128:(c + 1) * 128], kt_bf[:, c * L:(c + 1) * L], ident_bf),
                 r=["kt_bf", "cbf"], w=[bn])
        cp(ACT, ktok_bf[0:L, 0:nch, :], bkb[0:L, 0:nch * 128].rearrange("p (a c) -> p a c", a=nch), r=[bn], w=["ktok_bf"])
        if cfg.get('gla_lvl', 9) < 3:
            return
        yield
        dsb, dsn = pbank()
        for c in range(nch):
            for h in range(4):
                mm(dsb[32 * h:32 * h + 32, c * 64:(c + 1) * 64], ktok_bf[0:L, c, 32 * h:32 * h + 32],
                   vtok_bf[0:L, c, 64 * h:64 * h + 64], True, True, r=["ktok_bf", "vtok_bf"], w=[dsn], tp=(0, 32 * h))
        yield
        if sample:
            S.dma(Sin[:, :, :], d_gla0[l].rearrange("c p v -> p c v"), w=["Sin"], chan="gl0")
        elif blk == 0:
            S.op(DVE if P2D else POOL, lambda: (V if P2D else G).memset(Sst[:], 0.0), r=[], w=["Sst"])
        for c in range(nch):
            yield
            if sample:
                tt(DVE, Sall_bf[:, c, :, :], Sin[:, c, :].unsqueeze(1).to_broadcast([128, 4, 64]),
                   hm.unsqueeze(2).to_broadcast([128, 4, 64]), ALU.mult, r=["Sin", "consts"], w=["Sall_bf"])
                tt(DVE, Stmp[:], Sin[:, c, :], dsb[:, c * 64:(c + 1) * 64], ALU.add, r=["Sin", dsn], w=["Stmp"])
                ts(DVE, Stmp[:], Stmp[:], dec[:, c:c + 1], None, ALU.mult, None, r=["Stmp", "dec"], w=["Stmp"])
                S.dma(o_gla[l, 1 + c], Stmp[:], r=["Stmp"], chan="go")
            else:
                tt(DVE, Sall_bf[:, c, :, :], Sst[:].unsqueeze(1).to_broadcast([128, 4, 64]),
                   hm.unsqueeze(2).to_broadcast([128, 4, 64]), ALU.mult, r=["Sst", "consts"], w=["Sall_bf"])
                tt(DVE, Stmp[:], Sst[:], dsb[:, c * 64:(c + 1) * 64], ALU.add, r=["Sst", dsn], w=["Stmp"])
                ts(DVE, Sst[:], Stmp[:], dec[:, c:c + 1], None, ALU.mult, None, r=["Stmp", "dec"], w=["Sst"])
        if (not sample) and blk == NPB - 1:
            S.dma(o_gla[l, 0], Sst[:], r=["Sst"], chan="go")
        if cfg.get('gla_lvl', 9) < 4:
            return
        yield
        cpb = 512 // (4 * L)
        cm = consts[0:L, C_CM64:C_CM64 + L] if L == 64 else consts[0:L, C_CM16:C_CM16 + L]
        for g in range(nch // cpb if nch >= cpb else 1):
            ncg = min(cpb, nch)
            bk, bn = pbank()
            for cc in range(ncg):
                c = g * cpb + cc
                for h in range(4):
                    mm(bk[0:L, (cc * 4 + h) * L:(cc * 4 + h + 1) * L], ktx_bf[:, h, c * L:(c + 1) * L],
                       qt_bf[:, c * L:(c + 1) * L], True, True, r=["ktx_bf", "qt_bf"], w=[bn])
            tt(DVE, scT_bf[0:L, g * cpb:g * cpb + ncg, :, 0:L],
               bk[0:L, 0:ncg * 4 * L].rearrange("p (a h i) -> p a h i", a=ncg, h=4),
               cm.unsqueeze(1).unsqueeze(1).to_broadcast([L, ncg, 4, L]), ALU.mult, r=[bn, "consts"], w=["scT_bf"])
        if cfg.get('gla_lvl', 9) < 5:
            return
        yield
        obk = [(banks[4], "bank4"), (banks[5], "bank5")]
        for c in range(nch):
            yield
            for h in range(4):
                ob, on = obk[h // 2]
                po = 64 * (h % 2)
                mm(ob[po:po + 64, c * L:(c + 1) * L], vtok_bf[0:L, c, 64 * h:64 * h + 64], scT_bf[0:L, c, h, 0:L],
                   True, False, r=["vtok_bf", "scT_bf"], w=[on], tp=(0, po))
                mm(ob[po:po + 64, c * L:(c + 1) * L], Sall_bf[:, c, h, :], qt_bf[:, c * L:(c + 1) * L],
                   False, True, r=["Sall_bf", "qt_bf"], w=[on], tp=(0, po))
        if cfg.get('gla_lvl', 9) < 6:
            return
        yield
        for hp in range(2):
            ob, on = obk[hp]
            act(tb[0][:, hp, 0:N], ob[:, 0:N], AF.Square, r=[on], w=["tb0"])
        sgz = big[0]
        sigmoid_chain(sgz[:, 0:2, 0:N], pj[:, 5:7, 0:N], r=["pj5", "pj6"], w=["big0a"], t1=big[1][:, 0:2, 0:N], t1n="big1a")
        for hp in range(2):
            yield
            ob, on = obk[hp]
            bk, bn = pbank()
            mm(bk[:, 0:N], blk_bf, tb[0][:, hp, 0:N], True, True, r=["cbf", "tb0"], w=[bn])
            rsqrt_chain(tmp[4][:, 0:N], bk[:, 0:N], 1.0 / 64, r=[bn], w=["tmp4"], tmpap=tmp[5][:, 0:N], tmpname="tmp5")
            stt(DVE, tmp[6][:, 0:N], ob[:, 0:N], vecs[:, b + 12:b + 13], tmp[4][:, 0:N], ALU.mult, ALU.mult,
                r=[on, "vecs", "tmp4"], w=["tmp6"])
            tt(POOL, tmp[7][:, 0:N], pj[:, 5 + hp, 0:N], sgz[:, hp, 0:N], ALU.mult, r=["pj%d" % (5 + hp), "big0a"], w=["tmp7"])
            tt(DVE, ocat_bf[:, hp, 0:N], tmp[6][:, 0:N], tmp[7][:, 0:N], ALU.mult, r=["tmp6", "tmp7"], w=["ocat%d" % hp])

    def mla_qkv(l, blk, N, tok0, sample):
        b = l * VL
        act(tb[1][:, 0, 0:N], pj[:, 7, 0:N], AF.Square, r=["pj7"], w=["tb1"])
        act(tb[1][0:64, 1, 0:N], pj[0:64, 8, 0:N], AF.Square, r=["pj8"], w=["tb1"])
        bk, bn = pbank2()
        mm(bk[:, 0:N], ones_bf, tb[1][:, 0, 0:N], True, False, r=["cbf", "tb1"], w=[bn])
        mm(bk[:, 0:N], ones_bf[0:64, :], tb[1][0:64, 1, 0:N], False, True, r=["cbf", "tb1"], w=[bn])
        rsqrt_chain(big[0][:, 2, 0:N], bk[:, 0:N], 1.0 / 192, r=[bn], w=["big0b"], tmpap=big[0][:, 3, 0:N], tmpname="big0b")
        stt(DVE, cqn_bf[:, 0, 0:N], pj[:, 7, 0:N], vecs[:, b + 8:b + 9], big[0][:, 2, 0:N], ALU.mult, ALU.mult,
            r=["pj7", "big0b", "vecs"], w=["cqn_bf"])
        stt(DVE, cqn_bf[0:64, 1, 0:N], pj[0:64, 8, 0:N], vecs[0:64, b + 9:b + 10], big[0][0:64, 2, 0:N], ALU.mult, ALU.mult,
            r=["pj8", "big0b", "vecs"], w=["cqn_bf"])
        yield
        for h in range(4):
            yield
            bkA, bnA = pbank2()
            bkB, bnB = pbank2()
            for (bk_, bn_, off) in ((bkA, bnA, 2 * h * 96), (bkB, bnB, (2 * h + 1) * 96)):
                mm(bk_[0:96, 0:N], wuq_bf[:, 0, off:off + 96], cqn_bf[:, 0, 0:N], True, False, r=["wuq_bf", "cqn_bf"], w=[bn_])
                mm(bk_[0:96, 0:N], wuq_bf[0:64, 1, off:off + 96], cqn_bf[0:64, 1, 0:N], False, True, r=["wuq_bf", "cqn_bf"], w=[bn_])
            cp(ACT, Q_bf[0:64, h, 0:N], bkA[0:64, 0:N], r=[bnA], w=["Q%d" % h])
            tt(DVE, big[1][64:96, 2, 0:N], bkA[64:96, 0:N], ropec[64:96, 0:N], ALU.mult, r=[bnA, "ropec"], w=["big1b"])
            tt(DVE, big[1][64:96, 3, 0:N], bkB[64:96, 0:N], ropes[64:96, 0:N], ALU.mult, r=[bnB, "ropes"], w=["big1b"])
            tt(POOL, Q_bf[64:96, h, 0:N], big[1][64:96, 2, 0:N], big[1][64:96, 3, 0:N], ALU.add, r=["big1b", "big1b"], w=["Q%d" % h])
        yield
        tt(DVE, big[1][64:96, 2, 0:N], pj[64:96, 8, 0:N], ropec[64:96, 0:N], ALU.mult, r=["pj8", "ropec"], w=["big1b"])
        tt(DVE, big[1][64:96, 3, 0:N], pj[64:96, 4, 0:N], ropes[64:96, 0:N], ALU.mult, r=["pj4", "ropes"], w=["big1b"])
        tt(POOL, kpe_f[64:96, 0:N], big[1][64:96, 2, 0:N], big[1][64:96, 3, 0:N], ALU.add, r=["big1b", "big1b"], w=["kpe_f"])
        S.dma(o_kpe[l, :, tok0:tok0 + N], kpe_f[64:96, 0:N], r=["kpe_f"], chan="ko")
        yield
        act(tb[1][:, 0, 0:N], pj[:, 9, 0:N], AF.Square, r=["pj9"], w=["tb1"])
        bk, bn = pbank2()
        mm(bk[:, 0:N], ones_bf, tb[1][:, 0, 0:N], True, True, r=["cbf", "tb1"], w=[bn])
        rsqrt_chain(big[0][:, 2, 0:N], bk[:, 0:N], 1.0 / 128, r=[bn], w=["big0b"], tmpap=big[0][:, 3, 0:N], tmpname="big0b")
        stt(DVE, ckv_f[:, 0:N], pj[:, 9, 0:N], vecs[:, b + 10:b + 11], big[0][:, 2, 0:N], ALU.mult, ALU.mult,
            r=["pj9", "vecs", "big0b"], w=["ckv_f"])
        S.dma(o_ckv[l, :, tok0:tok0 + N], ckv_f[:, 0:N], r=["ckv_f"], chan="co")
        cp(ACT, ckv_bf[:, 0:N], ckv_f[:, 0:N], r=["ckv_f"], w=["ckv_bf"])

    def kres(h, j):
        return "K%d_%d" % (h, j)

    def attn_tail(l, N, Oaps, sumaps, onames, snames):
        for h in range(4):
            rc = tmp[4][:, 0:N]
            act(tmp[5][:, 0:N], sumaps[h], AF.Ln, r=[snames[h]], w=["tmp5"])
            act(rc, tmp[5][:, 0:N], AF.Exp, r=["tmp5"], w=["tmp4"], scale=-1.0)
            tt(DVE, tmp[6][:, 0:N], Oaps[h], rc, ALU.mult, r=[onames[h], "tmp4"], w=["tmp6"])
            tt(POOL, tmp[7][:, 0:N], pj[:, 10 + h, 0:N], big[0][:, h, 0:N], ALU.mult, r=["pj%d" % (10 + h)] + BIG0, w=["tmp7"])
            tt(DVE, ocat_bf[:, 2 + h, 0:N], tmp[6][:, 0:N], tmp[7][:, 0:N], ALU.mult, r=["tmp6", "tmp7"], w=["ocat%d" % (2 + h)])

    def mla_prompt_pre(l, blk, N, tok0):
        for h in range(4):
            bk, bn = pbank()
            mm(bk[0:64, 0:N], wukv_bf[:, 64 * h:64 * h + 64], ckv_bf[:, 0:N], True, True, r=["wukv_bf", "ckv_bf"], w=[bn])
            wr = [kres(h, 2 * blk), kres(h, 2 * blk + 1)]
            cp(ACT, Kbuf[0:64, h, tok0:tok0 + N], bk[0:64, 0:N], r=[bn], w=wr)
            cp(ACT, Kbuf[64:96, h, tok0:tok0 + N], kpe_f[64:96, 0:N], r=["kpe_f"], w=wr)
        for a in range(2):
            j = 2 * blk + a
            bk, bn = pbank()
            mm(bk[:, :], ckv_bf[:, a * 128:(a + 1) * 128], wukv_bf[:, 256:768], True, True, r=["ckv_bf", "wukv_bf"], w=[bn])
            cp(ACT, Vbuf[:, j, :], bk[:, :], r=[bn], w=["V%d" % j])
        sigmoid_chain(big[0][:, 0:4, 0:N], pj[:, 10:14, 0:N], r=["pj10", "pj11", "pj12", "pj13"], w=BIG0,
                      t1=big[1][:, 0:4, 0:N], t1n=BIG1)

    def mla_prompt_gen(l, blk, N, tok0):
        scale = 96 ** -0.5
        pti = 0
        njt = 2 * blk + 2
        items = [(h, j) for h in range(4) for j in range(njt)]

        def emit_scores(h, j):
            c0 = 0 if j <= 2 * blk else 128
            bk, bn = pbank()
            mm(bk[:, c0:N], Kbuf[0:96, h, j * 128:(j + 1) * 128], Q_bf[0:96, h, c0:N], True, True,
               r=[kres(h, j), "Q%d" % h], w=[bn])
            return bk, bn
        nxt = emit_scores(*items[0])
        for idx, (h, j) in enumerate(items):
            Ob, On = banks[4 + (h % 2)], "bank%d" % (4 + (h % 2))
            Sb, Sn = banks[6], "bank6a"
            if cfg.get('wi_heads'):
                On = "wiO%d" % h; Sn = "wiS%d" % h
            bk, bn = nxt
            if idx + 1 < len(items):
                nxt = emit_scores(*items[idx + 1])
            c0 = 0 if j <= 2 * blk else 128
            npt = cfg.get('wi_npt', 3)
            p_ = pT[pti % 3]; pn = "pT%d" % (pti % npt); pti += 1
            if j < 2 * blk:
                act(p_[:, 0:N], bk[:, 0:N], AF.Exp, r=[bn], w=[pn], scale=scale)
            else:
                a = j - 2 * blk
                d0 = a * 128
                act(p_[0:64, d0:d0 + 128], bk[0:64, d0:d0 + 128], AF.Exp, r=[bn], w=[pn], scale=scale)
                act(p_[64:128, d0 + 64:d0 + 128], bk[64:128, d0 + 64:d0 + 128], AF.Exp, r=[bn], w=[pn], scale=scale)
                S.op(DVE if P2D else POOL, lambda p_=p_, d0=d0: (V if P2D else G).memset(p_[64:128, d0:d0 + 64], 0.0), r=[], w=[pn])
                if a == 0:
                    act(p_[:, 128:N], bk[:, 128:N], AF.Exp, r=[bn], w=[pn], scale=scale)
            st = (j == 0)
            mm(Ob[:, c0:N], Vbuf[:, j, 128 * h:128 * h + 128], p_[:, c0:N], st, j == njt - 1, r=["V%d" % j, pn], w=[On])
            mm(Sb[:, c0:N], ones_bf, p_[:, c0:N], st, j == njt - 1, r=["cbf", pn], w=[Sn])
            yield
            if j == njt - 1:
                attn_tail_one(l, N, h, Ob[:, 0:N], Sb[:, 0:N], On, Sn)
                yield

    def attn_tail_one(l, N, h, Oap, sap, on, sn):
        if cfg.get('wi_tail'):
            sfx = "_wt%d" % h
            act(tmp[5][:, 0:N], sap, AF.Ln, r=[sn], w=["tmp5" + sfx])
            act(tmp[4][:, 0:N], tmp[5][:, 0:N], AF.Exp, r=["tmp5" + sfx], w=["tmp4" + sfx], scale=-1.0)
            tt(DVE, tmp[6][:, 0:N], Oap, tmp[4][:, 0:N], ALU.mult, r=[on, "tmp4" + sfx], w=["tmp6" + sfx])
            tt(POOL, tmp[7][:, 0:N], pj[:, 10 + h, 0:N], big[0][:, h, 0:N], ALU.mult, r=["pj%d" % (10 + h)] + BIG0, w=["tmp7" + sfx])
            tt(DVE, ocat_bf[:, 2 + h, 0:N], tmp[6][:, 0:N], tmp[7][:, 0:N], ALU.mult, r=["tmp6" + sfx, "tmp7" + sfx], w=["ocat%d" % (2 + h)])
            return
        rc = tmp[4][:, 0:N]
        act(tmp[5][:, 0:N], sap, AF.Ln, r=[sn], w=["tmp5"])
        act(rc, tmp[5][:, 0:N], AF.Exp, r=["tmp5"], w=["tmp4"], scale=-1.0)
        tt(DVE, tmp[6][:, 0:N], Oap, rc, ALU.mult, r=[on, "tmp4"], w=["tmp6"])
        tt(POOL, tmp[7][:, 0:N], pj[:, 10 + h, 0:N], big[0][:, h, 0:N], ALU.mult, r=["pj%d" % (10 + h)] + BIG0, w=["tmp7"])
        tt(DVE, ocat_bf[:, 2 + h, 0:N], tmp[6][:, 0:N], tmp[7][:, 0:N], ALU.mult, r=["tmp6", "tmp7"], w=["ocat%d" % (2 + h)])

    def mla_sample(l, N):
        scale = 96 ** -0.5
        NK = PAST + DEC_SEQ
        sigmoid_chain(big[0][:, 0:4, 0:N], pj[:, 10:14, 0:N], r=["pj10", "pj11", "pj12", "pj13"], w=BIG0,
                      t1=big[1][:, 0:4, 0:N], t1n=BIG1)
        Ob, On = banks[4], "bank4"
        Sb, Sn = banks[6], "bank6a"
        allK = [kres(h, j) for h in range(4) for j in range(9)]
        allV = ["V%d" % j for j in range(9)]
        for q in range(SPC):
            cs = slice(q * DEC_SEQ, (q + 1) * DEC_SEQ)
            S.dma(ckvall_bf[:, 0:PAST], d_ckvP[l, q], w=["ckvall_bf"], chan="cp0", queue="pool")
            cp(DVE, ckvall_bf[:, PAST:NK], ckv_bf[:, cs], r=["ckv_bf"], w=["ckvall_bf"])
            for h in range(4):
                S.dma(Kbuf[64:96, h, 0:PAST], d_kpeP[l, q], w=[kres(h, j) for j in range(8)], chan="cp%d" % (1 + h), queue="pool")
                cp(POOL, Kbuf[64:96, h, PAST:NK], kpe_f[64:96, cs], r=["kpe_f"], w=[kres(h, 8)])
                for (c0, c1) in ((0, 512), (512, 1024), (1024, NK)):
                    bk, bn = pbank()
                    mm(bk[0:64, 0:c1 - c0], wukv_bf[:, 64 * h:64 * h + 64], ckvall_bf[:, c0:c1], True, True,
                       r=["wukv_bf", "ckvall_bf"], w=[bn])
                    cp(ACT if h % 2 == 0 else DVE, Kbuf[0:64, h, c0:c1], bk[0:64, 0:c1 - c0], r=[bn],
                       w=[kres(h, j) for j in range(c0 // 128, (c1 + 127) // 128)])
            for j in range(9):
                nk = 128 if j < 8 else DEC_SEQ
                bk, bn = pbank()
                mm(bk[0:nk, :], ckvall_bf[:, j * 128:j * 128 + nk], wukv_bf[:, 256:768], True, True, r=["ckvall_bf", "wukv_bf"], w=[bn])
                cp(ACT if j % 2 == 0 else DVE, Vbuf[0:nk, j, :], bk[0:nk, :], r=[bn], w=["V%d" % j])
            for h in range(4):
                bk, bn = pbank()
                for j in range(9):
                    nk = 128 if j < 8 else DEC_SEQ
                    mm(bk[0:nk, j * 16:(j + 1) * 16], Kbuf[0:96, h, j * 128:j * 128 + nk], Q_bf[0:96, h, cs], True, True,
                       r=[kres(h, j), "Q%d" % h], w=[bn])
                p_ = pT[(q * 4 + h) % 3]; pn = "pT%d" % ((q * 4 + h) % 3)
                act(p_[:, 0:128], bk[:, 0:128], AF.Exp, r=[bn], w=[pn], scale=scale)
                act(p_[0:16, 128:144], bk[0:16, 128:144], AF.Exp, r=[bn], w=[pn], scale=scale)
                oc = slice(h * NS + q * DEC_SEQ, h * NS + (q + 1) * DEC_SEQ)
                for j in range(9):
                    nk = 128 if j < 8 else DEC_SEQ
                    mm(Ob[:, oc], Vbuf[0:nk, j, 128 * h:128 * h + 128], p_[0:nk, j * 16:(j + 1) * 16], j == 0, j == 8,
                       r=["V%d" % j, pn], w=[On])
                    mm(Sb[:, oc], ones_bf[0:nk, :], p_[0:nk, j * 16:(j + 1) * 16], j == 0, j == 8, r=["cbf", pn], w=[Sn, "bank6b"])
        for h in range(4):
            attn_tail_one(l, N, h, Ob[:, h * NS:(h + 1) * NS], Sb[:, h * NS:(h + 1) * NS], On, Sn)

    def s5_pre(l, blk, N, tok0, sample):
        if sample:
            S.dma(xpr[:, :, :], d_s5x0[l, 0].rearrange("q p s -> p q s"), w=["xpr"], chan="sx0")
            S.dma(xpi[:, :, :], d_s5x0[l, 1].rearrange("q p s -> p q s"), w=["xpi"], chan="sx1")
        elif blk == 0:
            S.op(DVE if P2D else POOL, lambda: (V if P2D else G).memset(xpr[:], 0.0), r=[], w=["xpr"])
            S.op(DVE if P2D else POOL, lambda: (V if P2D else G).memset(xpi[:], 0.0), r=[], w=["xpi"])

    def s5_chunks_gen(l, blk, N, tok0, sample):
        nseq = SPC if sample else 1
        L = DEC_SEQ if sample else SL
        W = nseq * L
        nchunk = N // W
        v4 = lambda ap: ap.rearrange("p (s q t) -> p s q t", s=8, q=nseq)
        tb4 = lambda tab: tab[:, :, 0:L].unsqueeze(2).to_broadcast([128, 8, nseq, L])
        T4 = [v4(sx[i][:, 0:8 * W]) for i in range(8)]
        F2 = [sx[i][:, 0:8 * W] for i in range(8)]
        xprv = xpr[:, 0:nseq, :].rearrange("p q s -> p s q"); xpiv = xpi[:, 0:nseq, :].rearrange("p q s -> p s q")
        rm2 = (RmS[:, :, :, :].rearrange("p s q t -> p (s q t)") if sample else Rm[:, :, :].rearrange("p s t -> p (s t)"))
        rmn = "RmS" if sample else "Rm"
        br, brn = banks[2], "bank2"
        bi, bin_ = banks[3], "bank3"
        br4 = v4(br[:, 0:8 * W]); bi4 = v4(bi[:, 0:8 * W])
        X, Y, Xn, Yn = 6, 7, sxn[6], sxn[7]
        V0, V1 = 4, 5

        def front_ops(c):
            c0 = c * W
            z0, z1 = 2 * (c % 2), 2 * (c % 2) + 1
            ops = []

            def bu():
                for s in range(8):
                    mm(br[:, s * W:(s + 1) * W], wB_bf[:, 0, s, :], u_bf[:, s // 4, c0:c0 + W], True, True, r=["wB_bf", "u_bf"], w=[brn])
                    mm(bi[:, s * W:(s + 1) * W], wB_bf[:, 1, s, :], u_bf[:, s // 4, c0:c0 + W], True, True, r=["wB_bf", "u_bf"], w=[bin_])
            ops.append(bu)
            ops.append(lambda: tt(DVE, T4[X], br4, tb4(T1r), ALU.mult, r=[brn, "T1r"], w=Xn))
            ops.append(lambda: tt(DVE, T4[Y], bi4, tb4(T1i), ALU.mult, r=[bin_, "T1i"], w=Yn))
            ops.append(lambda: tt(POOL, T4[z0], T4[X], T4[Y], ALU.subtract, r=Xn + Yn, w=sxn[z0]))
            ops.append(lambda: tt(DVE, T4[X], br4, tb4(T1i), ALU.mult, r=[brn, "T1i"], w=Xn))
            ops.append(lambda: tt(DVE, T4[Y], bi4, tb4(T1r), ALU.mult, r=[bin_, "T1r"], w=Yn))
            ops.append(lambda: tt(POOL, T4[z1], T4[X], T4[Y], ALU.add, r=Xn + Yn, w=sxn[z1]))
            return ops

        def back_ops(c):
            c0 = c * W
            z0, z1 = 2 * (c % 2), 2 * (c % 2) + 1
            rb = s5sm[:, 3, :].unsqueeze(2).to_broadcast([128, 8, nseq])
            xr4 = xr_bf[:, :, 0:W].rearrange("p s (q t) -> p s q t", q=nseq)
            xi4 = xi_bf[:, :, 0:W].rearrange("p s (q t) -> p s q t", q=nseq)
            ops = []
            ops.append(lambda: tt(POOL, rxr[:, :, 0:nseq], xprv, rb, ALU.mult, r=["xpr", "s5sm"], w=["rxr"]))
            ops.append(lambda: tt(POOL, T4[z0][:, :, :, 0], T4[z0][:, :, :, 0], rxr[:, :, 0:nseq], ALU.add, r=sxn[z0] + ["rxr"], w=sxn[z0]))
            ops.append(lambda: S.op(DVE, lambda: V.tensor_tensor_scan(out=F2[V0], data0=rm2, data1=F2[z0], initial=0.0, op0=ALU.mult, op1=ALU.add),
                                    r=sxn[z0] + [rmn], w=sxn[V0], cost=1200.0))
            ops.append(lambda: tt(POOL, rxi[:, :, 0:nseq], xpiv, rb, ALU.mult, r=["xpi", "s5sm"], w=["rxi"]))
            ops.append(lambda: tt(POOL, T4[z1][:, :, :, 0], T4[z1][:, :, :, 0], rxi[:, :, 0:nseq], ALU.add, r=sxn[z1] + ["rxi"], w=sxn[z1]))
            ops.append(lambda: S.op(DVE, lambda: V.tensor_tensor_scan(out=F2[V1], data0=rm2, data1=F2[z1], initial=0.0, op0=ALU.mult, op1=ALU.add),
                                    r=sxn[z1] + [rmn], w=sxn[V1], cost=1200.0))
            ops.append(lambda: tt(DVE, T4[z0], T4[V0], tb4(cosT), ALU.mult, r=sxn[V0] + ["cosT"], w=sxn[z0]))
            ops.append(lambda: tt(POOL, T4[z1], T4[V1], tb4(sinT), ALU.mult, r=sxn[V1] + ["sinT"], w=sxn[z1]))
            ops.append(lambda: tt(POOL, xprv, T4[z0][:, :, :, L - 1], T4[z1][:, :, :, L - 1], ALU.subtract, r=sxn[z0] + sxn[z1], w=["xpr"]))
            ops.append(lambda: tt(DVE, xr4, T4[z0], T4[z1], ALU.subtract, r=sxn[z0] + sxn[z1], w=["xr_bf"]))
            ops.append(lambda: tt(DVE, T4[z0], T4[V0], tb4(sinT), ALU.mult, r=sxn[V0] + ["sinT"], w=sxn[z0]))
            ops.append(lambda: tt(POOL, T4[z1], T4[V1], tb4(cosT), ALU.mult, r=sxn[V1] + ["cosT"], w=sxn[z1]))
            ops.append(lambda: tt(POOL, xpiv, T4[z0][:, :, :, L - 1], T4[z1][:, :, :, L - 1], ALU.add, r=sxn[z0] + sxn[z1], w=["xpi"]))
            ops.append(lambda: stt(DVE, xi4, T4[z0], -1.0, T4[z1], ALU.mult, ALU.subtract, r=sxn[z0] + sxn[z1], w=["xi_bf"]))

            def ymm():
                yb_, ybn_ = pbank()
                for uf in range(2):
                    yo = yb_[:, uf * W:(uf + 1) * W]
                    for s in range(4):
                        sg = 4 * uf + s
                        mm(yo, wC_bf[:, 0, sg, :], xr_bf[:, sg, 0:W], s == 0, False, r=["wC_bf", "xr_bf"], w=[ybn_])
                        mm(yo, wC_bf[:, 1, sg, :], xi_bf[:, sg, 0:W], False, False, r=["wC_bf", "xi_bf"], w=[ybn_])
                    mm(yo, dD_bf[:, uf, :], u_bf[:, uf, c0:c0 + W], False, True, r=["dD_bf", "u_bf"], w=[ybn_])
                cp(ACT, y_sb[:, :, c0:c0 + W], yb_[:, 0:2 * W].rearrange("p (a n) -> p a n", a=2), r=[ybn_], w=["ckvall_bf"])
            ops.append(ymm)
            return ops

        for o_ in front_ops(0):
            o_()
            yield
        for c in range(nchunk):
            fo = front_ops(c + 1) if c + 1 < nchunk else []
            bo = back_ops(c)
            i = j = 0
            while i < len(fo) or j < len(bo):
                for _ in range(2):
                    if j < len(bo):
                        bo[j](); j += 1
                if i < len(fo):
                    fo[i](); i += 1
                yield

    def s5_tail(l, blk, N, tok0, sample):
        ybn = "ckvall_bf"
        if sample:
            S.dma(o_s5[l, 0, 1:1 + SPC].rearrange("q p s -> p q s"), xpr[:, :, :], r=["xpr"], chan="so0")
            S.dma(o_s5[l, 1, 1:1 + SPC].rearrange("q p s -> p q s"), xpi[:, :, :], r=["xpi"], chan="so1")
        elif blk == NPB - 1:
            S.dma(o_s5[l, 0, 0], xpr[:, 0, :], r=["xpr"], chan="so0")
            S.dma(o_s5[l, 1, 0], xpi[:, 0, :], r=["xpi"], chan="so1")
        y2 = y_sb[:, :, 0:N]
        sq_ = big[1][:, 0:2, 0:N]; u_ = big[1][:, 2:4, 0:N]
        act(sq_, y2, AF.Square, r=[ybn], w=BIG1)
        act(sq_, sq_, AF.Identity, r=BIG1, w=BIG1, bias=1.0, scale=0.044715)
        tt(DVE, u_, sq_, y2, ALU.mult, r=BIG1 + [ybn], w=BIG1)
        ts(DVE, u_, u_, 20.0, -20.0, ALU.min, ALU.max, r=BIG1, w=BIG1)
        sigmoid_chain(u_, u_, r=BIG1, w=BIG1, t1=sq_, t1n=BIG1, scale=2.0 * GELU_C)
        tt(DVE, g5[:, :, 0:N], y2, u_, ALU.mult, r=BIG1 + [ybn], w=["g5"])
        cp(ACT, g5_bf[:, :, 0:N], g5[:, :, 0:N], r=["g5"], w=["g5_bf"])
        sigmoid_chain(big[0][:, 0:2, 0:N], pj[:, 16:18, 0:N], r=["pj16", "pj17"], w=["big0a"], t1=big[1][:, 0:2, 0:N], t1n=BIG1)
        for ot in range(2):
            bk, bn = pbank()
            for kt in range(2):
                mm(bk[:, 0:N], wglu_bf[:, kt, ot * 128:(ot + 1) * 128], g5_bf[:, kt, 0:N], kt == 0, kt == 1, r=["wglu_bf", "g5_bf"], w=[bn])
            sigmoid_chain(tmp[4][:, 0:N], bk[:, 0:N], r=[bn, "nvec"], w=["tmp4"], t1=tmp[5][:, 0:N], t1n="tmp5",
                          nbias=nvec[:, 4 * l + 1 + ot:4 * l + 2 + ot])
            tt(DVE, tmp[6][:, 0:N], g5[:, ot, 0:N], tmp[4][:, 0:N], ALU.mult, r=["g5", "tmp4"], w=["tmp6"])
            tt(POOL, tmp[7][:, 0:N], pj[:, 16 + ot, 0:N], big[0][:, ot, 0:N], ALU.mult, r=["pj%d" % (16 + ot), "big0a"], w=["tmp7"])
            tt(DVE, ocat_bf[:, 6 + ot, 0:N], tmp[6][:, 0:N], tmp[7][:, 0:N], ALU.mult, r=["tmp6", "tmp7"], w=["ocat%d" % (6 + ot)])

    def out_block(l, blk, N, tok0):
        oc = ["ocat%d" % i for i in range(8)]
        for (k0, k1) in ((0, 6), (6, 8)):
            for ot in range(8):
                bk, bn = pbank()
                for kt in range(k0, k1):
                    mm(bk[:, 0:N], wout_bf[:, kt, ot * 128:(ot + 1) * 128], ocat_bf[:, kt, 0:N], kt == k0, kt == k1 - 1,
                       r=["wout_bf%d" % kt, "ocat%d" % kt], w=[bn])
                xn = "%s_%d" % (cur["n"], ot)
                tt(DVE, cur["x"][:, ot, 0:N], cur["x"][:, ot, 0:N], bk[:, 0:N], ALU.add, r=[xn, bn], w=[xn])
                if l == 0 and k1 == 8:
                    S.dma(d_x1[:, ot, tok0:tok0 + N], cur["x"][:, ot, 0:N], r=[xn], w=["x1_%d_%d" % (blk, ot)], chan="x1o%d" % ot)
        if l == 0:
            pass
        else:
            sumsq_rstd(N, 1.0 / D)
            yn = ["pj%d" % i for i in range(8)]
            for kt in range(8):
                stt(DVE, ybuf[:, kt, 0:N], cur["x"][:, kt, 0:N], vecs[:, 2 * VL + kt:2 * VL + kt + 1], rstd[:, 0:N],
                    ALU.mult, ALU.mult, r=["%s_%d" % (cur["n"], kt), "vecs", "rstd"], w=[yn[kt]])
                S.dma(o_yT[:, kt, tok0:tok0 + N], ybuf[:, kt, 0:N], r=[yn[kt]], chan="yo%d" % kt)

    for l in range(c_layers):
        if c_setup:
            load_weights(l)
            s5_setup(l)
        for blk in c_blocks:
            sample = blk == NPB
            cur["x"] = xblks[blk % 2]; cur["n"] = "xblk%d" % (blk % 2)
            N = NS if sample else NB
            tok0 = blk * NB
            if 'proj' in c_st:
                proj_and_norm(l, blk, N, tok0)
            ga = gla_block(l, blk, N, tok0, DEC_SEQ if sample else 64, sample) if 'gla' in c_st else iter(())
            gq = mla_qkv(l, blk, N, tok0, sample) if 'qkv' in c_st else iter(())
            if cfg.get('seq'):
                for _ in ga:
                    pass
                for _ in gq:
                    pass
            else:
                da = dq = False
                while not (da and dq):
                    if not da:
                        try:
                            next(ga)
                        except StopIteration:
                            da = True
                    if not dq:
                        try:
                            next(gq)
                        except StopIteration:
                            dq = True
            if 'mla' in c_st and 's5' in c_st and not sample and not cfg.get('seq'):
                mla_prompt_pre(l, blk, N, tok0)
                s5_pre(l, blk, N, tok0, sample)
                g1 = mla_prompt_gen(l, blk, N, tok0)
                def _coarse(g, k):
                    i = 0
                    for _ in g:
                        i += 1
                        if i % k == 0:
                            yield
                kk = cfg.get('ilv_k', 1)
                g2 = _coarse(s5_chunks_gen(l, blk, N, tok0, sample), kk)
                n1 = 4 * (2 * blk + 3); n2 = ((N // 64) * 8 + 7) // kk
                done1 = done2 = False
                k1 = k2 = 0
                while not (done1 and done2):
                    if not done1 and (done2 or k1 * n2 <= k2 * n1):
                        try:
                            next(g1); k1 += 1
                        except StopIteration:
                            done1 = True
                    elif not done2:
                        try:
                            next(g2); k2 += 1
                        except StopIteration:
                            done2 = True
                s5_tail(l, blk, N, tok0, sample)
            else:
                if 'mla' in c_st:
                    if sample:
                        mla_sample(l, N)
                    else:
                        mla_prompt_pre(l, blk, N, tok0)
                        for _ in mla_prompt_gen(l, blk, N, tok0):
                            pass
                if 's5' in c_st:
                    s5_pre(l, blk, N, tok0, sample)
                    for _ in s5_chunks_gen(l, blk, N, tok0, sample):
                        pass
                    s5_tail(l, blk, N, tok0, sample)
            if 'out' in c_st:
                out_block(l, blk, N, tok0)
    if cfg.get('sched', True):
        mk = S.schedule()
    else:
        mk = 0.0
    stats = S.emit()
    stats['sim_us'] = mk / 1e3
    return nc, es, stats


def _win_colmap():
    m = -np.ones(WIN_COLS, dtype=np.int64)

    def put(tile, c0, src):
        src = np.asarray(list(src))
        m[tile * 128 + c0: tile * 128 + c0 + len(src)] = src
    put(0, 0, range(0, 128)); put(1, 0, range(128, 256))
    put(2, 0, range(256, 384)); put(3, 0, range(384, 512))
    put(4, 0, range(512, 528)); put(4, 64, [1104 + (r + 16) % 32 for r in range(32)])
    put(5, 0, range(528, 656)); put(6, 0, range(656, 784))
    put(7, 0, range(784, 912)); put(8, 0, range(912, 976)); put(8, 64, range(1104, 1136))
    put(9, 0, range(976, 1104))
    for i in range(4):
        put(10 + i, 0, range(1136 + 128 * i, 1136 + 128 * (i + 1)))
    put(14, 0, range(1648, 1776)); put(15, 0, range(1776, 1904))
    put(16, 0, range(1904, 2032)); put(17, 0, range(2032, 2160))
    return m


def _gather_cols(w, cmap):
    out = np.zeros(w.shape[:-1] + (len(cmap),), dtype=np.float32)
    ok = cmap >= 0
    out[..., ok] = w[..., cmap[ok]]
    return out


def _fm(v):
    return np.ascontiguousarray(v.reshape(-1, 128).T)


def _prep_shared(inp):
    f = lambda k: np.asarray(inp[k], dtype=np.float32)
    sh = {}
    w_in = f("w_in")
    cmap = _win_colmap()
    win = _gather_cols(w_in, cmap)
    sh["win"] = np.ascontiguousarray(win.reshape(2, 8, 128, WIN_COLS).transpose(0, 2, 1, 3))
    sh["wout"] = np.ascontiguousarray(f("w_out").reshape(2, 8, 128, D).transpose(0, 2, 1, 3))
    wuq = f("mla_w_uq")
    qmap = -np.ones(768, dtype=np.int64)
    for h in range(4):
        qmap[(2 * h) * 96:(2 * h) * 96 + 96] = np.arange(h * 96, h * 96 + 96)
        qmap[(2 * h + 1) * 96 + 64:(2 * h + 1) * 96 + 96] = [h * 96 + 64 + (r + 16) % 32 for r in range(32)]
    wq = _gather_cols(wuq, qmap)
    wq_p = np.zeros((2, 256, 768), np.float32); wq_p[:, :192] = wq
    sh["wuq"] = np.ascontiguousarray(wq_p.reshape(2, 2, 128, 768).transpose(0, 2, 1, 3))
    wukv = f("mla_w_ukv")
    kvmap = np.zeros(768, dtype=np.int64)
    for h in range(4):
        kvmap[h * 64:(h + 1) * 64] = np.arange(h * 192, h * 192 + 64)
        kvmap[256 + h * 128:256 + (h + 1) * 128] = np.arange(h * 192 + 64, h * 192 + 192)
    sh["wukv"] = np.ascontiguousarray(wukv[..., kvmap])
    sh["wgate"] = np.ascontiguousarray(f("gla_w_gate"))
    sh["wglu"] = np.ascontiguousarray(f("s5_w_glu").reshape(2, 2, 128, 256).transpose(0, 2, 1, 3))
    wB = np.zeros((2, 2, 128, 8, 128), np.float32)
    wC = np.zeros((2, 2, 128, 8, 128), np.float32)
    Bs = (f("s5_b_re"), f("s5_b_im"))
    Cs = (f("s5_c_re"), f("s5_c_im"))
    for sg in range(8):
        for gl in range(2):
            g = 2 * sg + gl
            r0 = (g % 8) * 16
            for ri in range(2):
                wB[:, ri, r0:r0 + 16, sg, gl * 64:(gl + 1) * 64] = Bs[ri][:, g].transpose(0, 2, 1)
                wC[:, ri, gl * 64:(gl + 1) * 64, sg, r0:r0 + 16] = Cs[ri][:, g].transpose(0, 2, 1)
    sh["wB"] = wB; sh["wC"] = wC
    dD = np.zeros((2, 128, 2, 128), np.float32)
    d = f("s5_d")
    for uf in range(2):
        dD[:, np.arange(128), uf, np.arange(128)] = d[:, uf * 128:(uf + 1) * 128]
    sh["dD"] = dD
    vec = np.zeros((128, NV), np.float32)
    lre, lim, ldt = f("s5_lambda_re"), f("s5_lambda_im"), f("s5_log_dt")
    for l in range(2):
        b = l * VL
        vec[:, b:b + 8] = _fm(f("ln_gain")[l])
        qg = np.zeros(256, np.float32); qg[:192] = f("mla_q_norm_gain")[l]
        vec[:, b + 8:b + 10] = _fm(qg)
        vec[:, b + 10] = f("mla_kv_norm_gain")[l]
        vec[:, b + 11] = f("gla_b_gate")[l]
        vec[:, b + 12] = np.tile(f("gla_norm_gain")[l], 2)
        vec[:, b + 13:b + 15] = _fm(f("s5_b_glu")[l])
        for sg in range(8):
            for gl in range(2):
                g = 2 * sg + gl
                vec[gl * 64:(gl + 1) * 64, b + 15 + sg] = lre[l, g]
                vec[gl * 64:(gl + 1) * 64, b + 23 + sg] = lim[l, g]
                vec[gl * 64:(gl + 1) * 64, b + 31 + sg] = ldt[l, g]
    vec[:, 2 * VL:2 * VL + 8] = _fm(f("final_gain"))
    sh["vecs"] = vec
    c = np.zeros((128, NCONST), np.float32)
    c[:, C_ID:C_ID + 128] = np.eye(128)
    c[:, C_ONES:C_ONES + 128] = 1.0
    c[0:64, C_BLK:C_BLK + 64] = 1.0; c[64:128, C_BLK + 64:C_BLK + 128] = 1.0
    jj = np.arange(64)[:, None]; ii = np.arange(64)[None, :]
    c[0:64, C_CM64:C_CM64 + 64] = (ii >= jj)
    c[0:16, C_CM16:C_CM16 + 16] = (ii[:, :16] >= jj[:16])
    c[:, C_GM64:C_GM64 + 256] = (np.arange(256) % 64 != 0)[None, :]
    c[:, C_GM16:C_GM16 + 64] = (np.arange(64) % 16 != 0)[None, :]
    c[:, C_HM:C_HM + 4] = (np.arange(128)[:, None] // 32 == np.arange(4)[None, :])
    sh["consts"] = c
    pos = np.concatenate([np.arange(SEQ), np.tile(PAST + np.arange(DEC_SEQ), SPC)]).astype(np.float32)
    inv = (np.float32(10000.0) ** (-np.arange(16, dtype=np.float32) / np.float32(16))).astype(np.float32)
    ang = (pos[None, :] * inv[:, None]).astype(np.float32)
    rope = np.zeros((2, 128, NTOK), np.float32)
    rope[0, 64:80] = np.cos(ang); rope[0, 80:96] = np.cos(ang)
    rope[1, 64:80] = -np.sin(ang); rope[1, 80:96] = np.sin(ang)
    sh["rope"] = rope
    return sh


def _prep_core(inp, c):
    f = lambda k: np.asarray(inp[k], dtype=np.float32)
    pc = {}
    xa = np.concatenate([f("x_prompt")[c], f("x_sample")[SPC * c:SPC * (c + 1)].reshape(NS, D)], axis=0)
    pc["xT"] = np.ascontiguousarray(xa.T.reshape(8, 128, NTOK).transpose(1, 0, 2))
    pc["gla0"] = np.ascontiguousarray(f("state_gla")[:, SPC * c:SPC * (c + 1)].reshape(2, SPC, 128, 64))
    s5 = np.zeros((2, 2, SPC, 128, 8), np.float32)
    for ri, k in enumerate(("state_s5_re", "state_s5_im")):
        st = f(k)[:, SPC * c:SPC * (c + 1)]
        s5[:, ri] = st.reshape(2, SPC, 8, 2, 64).transpose(0, 1, 3, 4, 2).reshape(2, SPC, 128, 8)
    pc["s5x0"] = s5
    pc["ckvP"] = np.ascontiguousarray(f("cache_mla_ckv")[:, SPC * c:SPC * (c + 1)].transpose(0, 1, 3, 2))
    pc["kpeP"] = np.ascontiguousarray(f("cache_mla_kpe")[:, SPC * c:SPC * (c + 1)].transpose(0, 1, 3, 2))
    return pc


_CACHE = {}


def kernel(**inputs):
    if "prog" not in _CACHE:
        _CACHE["prog"] = build_program()
    nc, es, stats = _CACHE["prog"]
    sh = _prep_shared(inputs)
    in_maps = []
    for c in range(N_CORES):
        m = dict(sh)
        m.update(_prep_core(inputs, c))
        in_maps.append(m)
    res = run_bass_kernel_spmd(nc, in_maps, core_ids=list(range(N_CORES)))
    B = N_CORES
    y_p = np.zeros((B, SEQ, D), np.float32); y_s = np.zeros((B * SPC, DEC_SEQ, D), np.float32)
    gla_p = np.zeros((2, B, 4, 32, 64), np.float32); gla_s = np.zeros((2, B * SPC, 4, 32, 64), np.float32)
    ckv_p = np.zeros((2, B, SEQ, 128), np.float32); ckv_s = np.zeros((2, B * SPC, DEC_SEQ, 128), np.float32)
    kpe_p = np.zeros((2, B, SEQ, 32), np.float32); kpe_s = np.zeros((2, B * SPC, DEC_SEQ, 32), np.float32)
    re_p = np.zeros((2, B, 16, 64), np.float32); im_p = np.zeros((2, B, 16, 64), np.float32)
    re_s = np.zeros((2, B * SPC, 16, 64), np.float32); im_s = np.zeros((2, B * SPC, 16, 64), np.float32)

    def unstate(a):
        return a.reshape(2, 64, 8).transpose(2, 0, 1).reshape(16, 64)
    for c in range(N_CORES):
        r = res.results[c]
        ya = np.asarray(r["yT"]).transpose(1, 0, 2).reshape(D, NTOK).T
        y_p[c] = ya[:SEQ]; y_s[SPC * c:SPC * (c + 1)] = ya[SEQ:].reshape(SPC, DEC_SEQ, D)
        g = np.asarray(r["glaO"])
        gla_p[:, c] = g[:, 0].reshape(2, 4, 32, 64)
        gla_s[:, SPC * c:SPC * (c + 1)] = g[:, 1:].reshape(2, SPC, 4, 32, 64)
        ck = np.asarray(r["ckvO"]); kp = np.asarray(r["kpeO"])
        ckv_p[:, c] = ck[:, :, :SEQ].transpose(0, 2, 1)
        kpe_p[:, c] = kp[:, :, :SEQ].transpose(0, 2, 1)
        ckv_s[:, SPC * c:SPC * (c + 1)] = ck[:, :, SEQ:].reshape(2, 128, SPC, DEC_SEQ).transpose(0, 2, 3, 1)
        kpe_s[:, SPC * c:SPC * (c + 1)] = kp[:, :, SEQ:].reshape(2, 32, SPC, DEC_SEQ).transpose(0, 2, 3, 1)
        s5 = np.asarray(r["s5O"])
        for l in range(2):
            re_p[l, c] = unstate(s5[l, 0, 0]); im_p[l, c] = unstate(s5[l, 1, 0])
            for q in range(SPC):
                re_s[l, SPC * c + q] = unstate(s5[l, 0, 1 + q]); im_s[l, SPC * c + q] = unstate(s5[l, 1, 1 + q])
    return (y_p, y_s, gla_p, ckv_p, kpe_p, re_p, im_p, gla_s, ckv_s, kpe_s, re_s, im_s)
```

```python
import math
import numpy as np
from contextlib import ExitStack
import concourse.bass as bass
import concourse.mybir as mybir
from concourse.bass_utils import run_bass_kernel_spmd

F32 = mybir.dt.float32
BF16 = mybir.dt.bfloat16
ALU = mybir.AluOpType
AF = mybir.ActivationFunctionType

N_CORES = 8
D = 1024
SEQ = 2048
DEC_SEQ = 16
SPC = 4
NS = SPC * DEC_SEQ
NTOK = SEQ + NS
PAST = 1024
NB = 256
NPB = SEQ // NB
NT_IN = 18
WIN_COLS = NT_IN * 128
VL = 39
NV = 2 * VL + 8
EPS = 1e-6
MAGIC = 12582912.0
GELU_C = math.sqrt(2.0 / math.pi)

C_ID, C_ONES, C_BLK, C_CM64, C_CM16, C_GM64, C_GM16 = 0, 128, 256, 384, 448, 464, 720
C_HM = 784
NCONST = 788


class _Op:
    __slots__ = ("eng", "fn", "deps", "dma", "chan", "signal", "sigval", "chanval", "cost", "lat", "tag")

    def __init__(self, eng, fn, dma, chan, cost, lat):
        self.eng = eng; self.fn = fn; self.deps = {}; self.dma = dma; self.chan = chan
        self.signal = False; self.sigval = 0; self.chanval = 0
        self.cost = cost
        self.lat = lat


class Sched:
    def __init__(self, nc, es):
        self.nc = nc
        self.es = es
        self.ops = []
        self.last_w = {}
        self.readers = {}
        self.engobj = {"pe": nc.tensor, "act": nc.scalar, "dve": nc.vector, "pool": nc.gpsimd, "sp": nc.sync}
        self.chan_last = {}

    def sb(self, name, shape, dtype=F32):
        return self.es.enter_context(self.nc.sbuf_tensor("s_" + name, list(shape), dtype))

    def ps(self, name, shape, dtype=F32):
        return self.es.enter_context(self.nc.psum_tensor(name, list(shape), dtype))

    def op(self, eng, fn, r=(), w=(), dma=False, chan=None, cost=300.0, lat=0.0):
        idx = len(self.ops)
        o = _Op(eng, fn, dma, chan, cost, lat)
        o.tag = ""
        if getattr(self, "want_tags", False):
            import sys as _sys
            f = _sys._getframe(1)
            for _ in range(8):
                if f is None:
                    break
                nm = f.f_code.co_name
                if nm in ("proj_and_norm", "sumsq_rstd", "gla_block", "mla_qkv", "mla_prompt_pre", "mla_prompt_gen", "attn_tail_one", "mla_sample",
                          "s5_pre", "s5_chunks_gen", "s5_tail", "out_block", "load_weights", "s5_setup"):
                    o.tag = nm
                    break
                f = f.f_back
        dres = self.__dict__.setdefault("dres", {})
        for res in r:
            j = self.last_w.get(res)
            if j is not None:
                o.deps[j] = "RAW"; dres[(idx, j)] = res
        for res in w:
            j = self.last_w.get(res)
            if j is not None:
                o.deps[j] = "RAW" if o.deps.get(j) == "RAW" else "WAW"; dres.setdefault((idx, j), res)
            for j in self.readers.get(res, ()):
                if j != idx and j not in o.deps:
                    o.deps[j] = "WAR"; dres[(idx, j)] = res
        if dma:
            j = self.chan_last.get(chan)
            if j is not None and j not in o.deps:
                o.deps[j] = "CHAN"
            self.chan_last[chan] = idx
        for res in r:
            self.readers.setdefault(res, []).append(idx)
        for res in w:
            self.last_w[res] = idx
            self.readers[res] = []
        self.ops.append(o)
        return idx

    def dma(self, out, in_, r=(), w=(), chan=None, queue="sp", **kw):
        eng = self.engobj[queue]
        try:
            nbytes = float(out.nbytes())
        except Exception:
            nbytes = 65536.0
        return self.op(queue, lambda: eng.dma_start(out=out, in_=in_, **kw), r=r, w=w, dma=True, chan=chan,
                       cost=(120.0 if queue == "sp" else 1000.0), lat=2500.0 + nbytes / 150.0)

    def schedule(self):
        import heapq
        ops = self.ops
        n = len(ops)
        succ = [[] for _ in range(n)]
        ndep = [0] * n
        for i, o in enumerate(ops):
            ndep[i] = len(o.deps)
            for j in o.deps:
                succ[j].append(i)
        engs = ["pe", "act", "dve", "pool", "sp"]
        free = {e: 0.0 for e in engs}
        fut = {e: [] for e in engs}
        avail = {e: [] for e in engs}
        ready_t = [0.0] * n
        fin = [0.0] * n
        start = [0.0] * n
        for i, o in enumerate(ops):
            if ndep[i] == 0:
                heapq.heappush(fut[o.eng], (0.0, i))
        done = 0
        order = []
        SYNC = 120.0
        binder = {}
        blame = {}
        while done < n:
            best = None
            for e in engs:
                f = fut[e]; a = avail[e]
                while f and f[0][0] <= free[e]:
                    heapq.heappush(a, heapq.heappop(f)[1])
                if a:
                    cand = (free[e], a[0], e, True)
                elif f:
                    cand = (f[0][0], f[0][1], e, False)
                else:
                    continue
                if best is None or cand[:2] < best[:2]:
                    best = cand
            st, i, e, from_avail = best
            if from_avail:
                heapq.heappop(avail[e])
            else:
                heapq.heappop(fut[e])
            o = ops[i]
            if st > free[e] + 1.0 and i in binder:
                jb = binder[i]
                key = (getattr(self, "dres", {}).get((i, jb), "?"), o.deps.get(jb, "?"), o.eng)
                blame[key] = blame.get(key, 0.0) + (st - free[e])
            start[i] = st
            free[e] = st + o.cost
            fin[i] = st + o.cost + o.lat
            order.append(i)
            done += 1
            for k in succ[i]:
                rt = fin[i] + (((200.0 if e == "pe" else SYNC)) if ops[k].eng != e or o.dma else (20.0 if e == "pe" else 110.0))
                if rt > ready_t[k]:
                    ready_t[k] = rt
                    binder[k] = i
                ndep[k] -= 1
                if ndep[k] == 0:
                    heapq.heappush(fut[ops[k].eng], (ready_t[k], k))
        pos = {old: new for new, old in enumerate(order)}
        new_ops = [ops[i] for i in order]
        for o in new_ops:
            o.deps = {pos[j]: kind for j, kind in o.deps.items()}
        self.ops = new_ops
        self.blame = blame
        self.sim_start = {id(ops[i]): start[i] for i in range(n)}
        self.sim_binder = {id(ops[k]): ops[j] for k, j in binder.items()}
        self.sim_makespan = max(fin) if fin else 0.0
        return self.sim_makespan

    def emit(self):
        nc = self.nc
        ops = self.ops
        engs = ["pe", "act", "dve", "pool", "sp"]
        need = []
        for i, o in enumerate(ops):
            lst = []
            best = {}
            for j, kind in o.deps.items():
                pj = ops[j]
                if pj.dma:
                    lst.append(j)
                    continue
                if pj.eng == o.eng and not o.dma:
                    if o.eng == "pe":
                        continue
                if pj.eng not in best or j > best[pj.eng]:
                    best[pj.eng] = j
            for e, j in best.items():
                lst.append(j)
                ops[j].signal = True
            need.append(lst)
        last_on = {}
        for i, o in enumerate(ops):
            if not o.dma:
                last_on[o.eng] = i
        for e, i in last_on.items():
            ops[i].signal = True
        sem = {e: self.es.enter_context(nc.semaphore("sem_" + e)) for e in engs}
        chans = sorted({o.chan for o in ops if o.dma})
        csem = {c: self.es.enter_context(nc.semaphore("dch_" + str(c))) for c in chans}
        cnt = {e: 0 for e in engs}
        ccnt = {c: 0 for c in chans}
        known = {e: {} for e in engs}
        nwaits = 0
        for i, o in enumerate(ops):
            eo = self.engobj[o.eng]
            kn = known[o.eng]
            for j in need[i]:
                pj = ops[j]
                if pj.dma:
                    key = ("c", pj.chan); val = pj.chanval; s = csem[pj.chan]
                else:
                    key = ("e", pj.eng); val = pj.sigval; s = sem[pj.eng]
                if kn.get(key, 0) >= val:
                    continue
                eo.wait_ge(s, val)
                nwaits += 1
                kn[key] = val
            ins = o.fn()
            if o.dma:
                ccnt[o.chan] += 16
                o.chanval = ccnt[o.chan]
                ins.then_inc(csem[o.chan], 16)
            elif o.signal:
                cnt[o.eng] += 1
                o.sigval = cnt[o.eng]
                ins.then_inc(sem[o.eng], 1)
        for c in chans:
            if ccnt[c] > 0 and known["sp"].get(("c", c), 0) < ccnt[c]:
                nc.sync.wait_ge(csem[c], ccnt[c])
        for e in ("pe", "act", "dve", "pool"):
            if cnt[e] > 0:
                nc.sync.wait_ge(sem[e], cnt[e])
        fin = self.es.enter_context(nc.semaphore("sem_fin"))
        for e in ("pe", "act", "dve", "pool"):
            self.engobj[e].nop().then_inc(fin, 1)
        nc.sync.wait_ge(fin, 4)
        return dict(n_ops=len(ops), n_waits=nwaits, signals=dict(cnt), n_chans=len(chans))


def build_program(cfg=None):
    cfg = cfg or {}
    c_layers = cfg.get('layers', 2); c_blocks = cfg.get('blocks', list(range(NPB + 1))); c_st = cfg.get('stages', ('proj', 'gla', 'qkv', 'mla', 's5', 'out')); c_setup = cfg.get('setup', True)
    nc = bass.Bass("TRN2", target_bir_lowering=False)
    es = ExitStack()
    S = Sched(nc, es)
    S.want_tags = bool(cfg.get('tags'))
    PE, ACT, DVE, POOL = "pe", "act", "dve", "pool"
    T, V, A, G = nc.tensor, nc.vector, nc.scalar, nc.gpsimd

    def din(name, shape):
        return nc.dram_tensor(name, list(shape), F32, kind="ExternalInput").ap()

    def dout(name, shape):
        return nc.dram_tensor(name, list(shape), F32, kind="ExternalOutput").ap()

    d_xT = din("xT", [128, 8, NTOK])
    d_win = din("win", [2, 128, 8, WIN_COLS])
    d_wout = din("wout", [2, 128, 8, D])
    d_wuq = din("wuq", [2, 128, 2, 768])
    d_wukv = din("wukv", [2, 128, 768])
    d_wgate = din("wgate", [2, 16, 128])
    d_wglu = din("wglu", [2, 128, 2, 256])
    d_wB = din("wB", [2, 2, 128, 8, 128])
    d_wC = din("wC", [2, 2, 128, 8, 128])
    d_dD = din("dD", [2, 128, 2, 128])
    d_vecs = din("vecs", [128, NV])
    d_consts = din("consts", [128, NCONST])
    d_rope = din("rope", [2, 128, NTOK])
    d_gla0 = din("gla0", [2, SPC, 128, 64])
    d_s5x0 = din("s5x0", [2, 2, SPC, 128, 8])
    d_ckvP = din("ckvP", [2, SPC, 128, PAST])
    d_kpeP = din("kpeP", [2, SPC, 32, PAST])
    o_yT = dout("yT", [128, 8, NTOK])
    o_gla = dout("glaO", [2, 1 + SPC, 128, 64])
    o_ckv = dout("ckvO", [2, 128, NTOK])
    o_kpe = dout("kpeO", [2, 32, NTOK])
    o_s5 = dout("s5O", [2, 2, 1 + SPC, 128, 8])
    d_x1 = nc.dram_tensor("x1scr", [128, 8, NTOK], F32).ap()

    sb = S.sb
    consts = sb("consts", [128, NCONST])
    cbf = sb("cbf", [128, 384], BF16)
    vecs = sb("vecs", [128, NV])
    nvec = sb("nvec", [128, 8])
    win_bf = sb("win_bf", [128, 8, WIN_COLS], BF16)
    wout_bf = sb("wout_bf", [128, 8, D], BF16)
    wuq_bf = sb("wuq_bf", [128, 2, 768], BF16)
    wukv_bf = sb("wukv_bf", [128, 768], BF16)
    wgate_bf = sb("wgate_bf", [16, 128], BF16)
    wglu_bf = sb("wglu_bf", [128, 2, 256], BF16)
    wB_bf = sb("wB_bf", [128, 2, 8, 128], BF16)
    wC_bf = sb("wC_bf", [128, 2, 8, 128], BF16)
    dD_bf = sb("dD_bf", [128, 2, 128], BF16)

    s5sm = sb("s5sm", [128, 16, 8])
    SL = 64
    cosT = sb("cosT", [128, 8, SL]); sinT = sb("sinT", [128, 8, SL])
    T1r = sb("T1r", [128, 8, SL]); T1i = sb("T1i", [128, 8, SL]); Rm = sb("Rm", [128, 8, SL])
    RmS = sb("RmS", [128, 8, SPC, DEC_SEQ])
    xpr = sb("xpr", [128, SPC, 8]); xpi = sb("xpi", [128, SPC, 8])
    rxr = sb("rxr", [128, 8, SPC]); rxi = sb("rxi", [128, 8, SPC])
    xr_bf = sb("xr_bf", [128, 8, SL], BF16); xi_bf = sb("xi_bf", [128, 8, SL], BF16)
    xblks = [sb("xblk%d" % i, [128, 8, NB]) for i in range(2)]
    cur = {"x": xblks[0], "n": "xblk0"}
    sqr = [sb("sqr%d" % i, [128, NB], BF16) for i in range(4)]
    h_bf = sb("h_bf", [128, 8, NB], BF16)
    rstd = sb("rstd", [128, NB])
    pj = sb("pj", [128, NT_IN, NB])
    s5t = [pj[:, 2 * i:2 * i + 2, :].rearrange("p a n -> p (a n)") for i in range(6)]
    s5tn = [["pj%d" % (2 * i), "pj%d" % (2 * i + 1)] for i in range(6)]
    tmpAll = sb("tmpAll", [128, 8, NB])
    tmp = [tmpAll[:, i, :] for i in range(8)]
    _pp = [(0, 1), (2, 3), (4, 5), (6, 7), (8, 9), (14, 15)]
    sx = [pj[:, a:a + 2, :].rearrange("p a n -> p (a n)") for (a, _) in _pp] + \
         [tmpAll[:, 0:2, :].rearrange("p a n -> p (a n)"), tmpAll[:, 2:4, :].rearrange("p a n -> p (a n)")]
    sxn = [["pj%d" % a, "pj%d" % b_] for (a, b_) in _pp] + [["tmp0", "tmp1"], ["tmp2", "tmp3"]]
    tb = [sb("tb%d" % i, [128, 2, NB], BF16) for i in range(2)]
    glr_bf = sb("glr_bf", [16, NB], BF16)
    qt_bf = sb("qt_bf", [128, NB], BF16); kt_bf = sb("kt_bf", [128, NB], BF16)
    dec = sb("dec", [128, 4])
    vtok_bf = sb("vtok_bf", [64, 4, 256], BF16); ktok_bf = sb("ktok_bf", [64, 4, 128], BF16)
    scT_bf = sb("scT_bf", [64, 4, 4, 64], BF16)
    Sst = sb("Sst", [128, 64]); Stmp = sb("Stmp", [128, 64]); Sin = sb("Sin", [128, 4, 64]); Sall_bf = sb("Sall_bf", [128, 4, 4, 64], BF16)
    ktx_bf = sb("ktx_bf", [128, 4, NB], BF16)
    ocat_bf = sb("ocat_bf", [128, 8, NB], BF16)
    cqn_bf = sb("cqn_bf", [128, 2, NB], BF16)
    Q_bf = sb("Q_bf", [128, 4, NB], BF16)
    ropec = sb("ropec", [128, NB]); ropes = sb("ropes", [128, NB])
    ckv_f = sb("ckv_f", [128, NB]); ckv_bf = sb("ckv_bf", [128, NB], BF16)
    kpe_f = sb("kpe_f", [128, NB])
    Kbuf = sb("Kbuf", [128, 4, SEQ], BF16)
    Vbuf = sb("Vbuf", [128, 16, 512], BF16)
    pT = [sb("pT%d" % i, [128, NB], BF16) for i in range(3)]
    ckvall_bf = sb("ckvall_bf", [128, PAST + DEC_SEQ], BF16)
    u_bf = sb("u_bf", [128, 2, NB], BF16)
    y_sb = ckvall_bf[:, 0:1024].bitcast(F32).rearrange("p (a n) -> p a n", a=2)
    g5 = sb("g5", [128, 2, NB]); g5_bf = sb("g5_bf", [128, 2, NB], BF16)
    big = [sb("big%d" % i, [128, 4, NB]) for i in range(2)]
    ybuf = pj

    banks = [S.ps("bank%d" % i, [128, 512]) for i in range(8)]
    pool_i = [0]

    def pbank():
        npool = cfg.get('wi_pool', 3)
        i = pool_i[0] % npool
        pool_i[0] += 1
        if i >= 3:
            return banks[i % 2], "wibank%d" % i
        bi_ = (0, 1, 7)[i]
        return banks[bi_], "bank%d" % bi_

    pool2_i = [0]

    def pbank2():
        i = (2, 3, 6)[pool2_i[0] % 3]
        pool2_i[0] += 1
        return banks[i], ("bank%d" % i if i != 6 else "bank6a")

    BIG0 = ["big0a", "big0b"]; BIG1 = ["big1a", "big1b"]

    def fsz(ap):
        n = 1
        for d in ap.shape[1:]:
            n *= int(d)
        return float(n)

    def ecost(eng, ap, per=None):
        n = fsz(ap)
        if eng == ACT:
            return 190.0 + 0.70 * n
        if eng == DVE:
            return 150.0 + 0.86 * (per if per is not None else 1.05) * n
        return 280.0 + (per if per is not None else 2.2) * n

    def act(out, in_, func, r, w, bias=0.0, scale=1.0):
        S.op(ACT, lambda: A.activation(out=out, in_=in_, func=func, bias=bias, scale=scale), r=r, w=w, cost=ecost(ACT, out))

    P2D = cfg.get('pool_to_dve', True)

    def tt(eng, out, in0, in1, op, r, w):
        if P2D and eng == POOL:
            eng = DVE
        e = V if eng == DVE else G
        S.op(eng, lambda: e.tensor_tensor(out=out, in0=in0, in1=in1, op=op), r=r, w=w, cost=ecost(eng, out))

    def ts(eng, out, in0, s1, s2, op0, op1, r, w):
        if P2D and eng == POOL:
            eng = DVE
        e = V if eng == DVE else G
        if s2 is None:
            S.op(eng, lambda: e.tensor_scalar(out=out, in0=in0, scalar1=s1, scalar2=None, op0=op0), r=r, w=w, cost=ecost(eng, out, 0.6))
        else:
            S.op(eng, lambda: e.tensor_scalar(out=out, in0=in0, scalar1=s1, scalar2=s2, op0=op0, op1=op1), r=r, w=w, cost=ecost(eng, out, 0.6))

    def stt(eng, out, in0, scalar, in1, op0, op1, r, w):
        e = V if eng == DVE else G
        S.op(eng, lambda: e.scalar_tensor_tensor(out=out, in0=in0, scalar=scalar, in1=in1, op0=op0, op1=op1), r=r, w=w, cost=ecost(eng, out))

    def cp(eng, out, in_, r, w):
        if P2D and eng == POOL:
            eng = DVE
        if eng == ACT:
            S.op(ACT, lambda: A.copy(out=out, in_=in_), r=r, w=w, cost=ecost(ACT, out))
        else:
            e = V if eng == DVE else G
            S.op(eng, lambda: e.tensor_copy(out=out, in_=in_), r=r, w=w, cost=ecost(eng, out, 0.8))

    def mm(out, lhsT, rhs, start, stop, r, w, tp=None):
        c = 35.0 + 0.58 * max(64.0, fsz(out))
        if tp is None:
            S.op(PE, lambda: T.matmul(out, lhsT=lhsT, rhs=rhs, start=start, stop=stop), r=r, w=w, cost=c)
        else:
            S.op(PE, lambda: T.matmul(out, lhsT=lhsT, rhs=rhs, start=start, stop=stop, tile_position=tp), r=r, w=w, cost=c)

    def rsqrt_chain(out, ps_in, scale, r, w, tmpap, tmpname):
        act(tmpap, ps_in, AF.Ln, r=r, w=[tmpname], bias=EPS, scale=scale)
        act(out, tmpap, AF.Exp, r=[tmpname], w=w, scale=-0.5)

    def sigmoid_chain(out, in_, r, w, t1, t1n, scale=1.0, nbias=0.0):
        t1n = [t1n] if isinstance(t1n, str) else list(t1n)
        act(t1, in_, AF.Exp, r=r, w=t1n, bias=nbias, scale=-scale)
        act(t1, t1, AF.Ln, r=t1n, w=t1n, bias=1.0)
        act(out, t1, AF.Exp, r=t1n, w=w, scale=-1.0)

    ident_bf = cbf[:, 0:128]
    ones_bf = cbf[:, 128:256]
    blk_bf = cbf[:, 256:384]

    S.dma(consts[:], d_consts[:, :], w=["consts"], chan="ld0")
    S.dma(vecs[:], d_vecs[:, :], w=["vecs"], chan="ld1")
    cp(DVE, cbf[:], consts[:, 0:384], r=["consts"], w=["cbf"])
    for l in range(2):
        b = l * VL
        ts(DVE, nvec[:, 4 * l:4 * l + 1], vecs[:, b + 11:b + 12], -1.0, None, ALU.mult, None, r=["vecs"], w=["nvec"])
        ts(DVE, nvec[:, 4 * l + 1:4 * l + 3], vecs[:, b + 13:b + 15], -1.0, None, ALU.mult, None, r=["vecs"], w=["nvec"])

    wl_i = [0]

    def wdma(dst, src_ap, wname):
        c = "wl%d" % (wl_i[0] % 6)
        wl_i[0] += 1
        S.dma(dst, src_ap, w=[wname], chan=c, queue="pool")

    def load_weights(l):
        for kt in range(8):
            wdma(win_bf[:, kt, :], d_win[l, :, kt, :], "win_bf%d" % kt)
        for kt in range(8):
            wdma(wout_bf[:, kt, :], d_wout[l, :, kt, :], "wout_bf%d" % kt)
        for a in range(2):
            wdma(wuq_bf[:, a, :], d_wuq[l, :, a, :], "wuq_bf")
        wdma(wukv_bf[:], d_wukv[l], "wukv_bf")
        wdma(wglu_bf[:].rearrange("p a c -> p (a c)"), d_wglu[l].rearrange("p a c -> p (a c)"), "wglu_bf")
        wdma(dD_bf[:].rearrange("p a c -> p (a c)"), d_dD[l].rearrange("p a c -> p (a c)"), "dD_bf")
        wdma(wgate_bf[:], d_wgate[l], "wgate_bf")
        for ri in range(2):
            wdma(wB_bf[:, ri].rearrange("p a c -> p (a c)"), d_wB[l, ri].rearrange("p a c -> p (a c)"), "wB_bf")
            wdma(wC_bf[:, ri].rearrange("p a c -> p (a c)"), d_wC[l, ri].rearrange("p a c -> p (a c)"), "wC_bf")

    def s5_setup(l):
        b = l * VL
        lam_re = vecs[:, b + 15:b + 23]; lam_im = vecs[:, b + 23:b + 31]; log_dt = vecs[:, b + 31:b + 39]
        sm = lambda i: s5sm[:, i, :]
        R = ["vecs", "s5sm"]; W = ["s5sm"]
        act(sm(0), log_dt, AF.Exp, r=R, w=W)
        tt(DVE, sm(1), lam_re, sm(0), ALU.mult, r=R, w=W)
        tt(DVE, sm(2), lam_im, sm(0), ALU.mult, r=R, w=W)
        act(sm(3), sm(1), AF.Exp, r=R, w=W)
        for (dst, off) in ((4, 0.0), (5, math.pi / 2)):
            ts(DVE, sm(6), sm(2), off, None, ALU.add, None, r=R, w=W)
            ts(DVE, sm(7), sm(6), float(1 / (2 * math.pi)), MAGIC, ALU.mult, ALU.add, r=R, w=W)
            ts(DVE, sm(7), sm(7), MAGIC, float(-2 * math.pi), ALU.subtract, ALU.mult, r=R, w=W)
            tt(DVE, sm(6), sm(6), sm(7), ALU.add, r=R, w=W)
            act(sm(dst), sm(6), AF.Sin, r=R, w=W)
        tt(DVE, sm(6), sm(3), sm(5), ALU.mult, r=R, w=W)
        tt(DVE, sm(7), sm(3), sm(4), ALU.mult, r=R, w=W)
        ts(DVE, sm(8), sm(6), -1.0, None, ALU.add, None, r=R, w=W)
        tt(DVE, sm(9), sm(8), lam_re, ALU.mult, r=R, w=W)
        tt(DVE, sm(10), sm(7), lam_im, ALU.mult, r=R, w=W)
        tt(DVE, sm(9), sm(9), sm(10), ALU.add, r=R, w=W)
        tt(DVE, sm(10), sm(7), lam_re, ALU.mult, r=R, w=W)
        tt(DVE, sm(11), sm(8), lam_im, ALU.mult, r=R, w=W)
        tt(DVE, sm(10), sm(10), sm(11), ALU.subtract, r=R, w=W)
        tt(DVE, sm(11), lam_re, lam_re, ALU.mult, r=R, w=W)
        tt(DVE, sm(12), lam_im, lam_im, ALU.mult, r=R, w=W)
        tt(DVE, sm(11), sm(11), sm(12), ALU.add, r=R, w=W)
        S.op(DVE, lambda: V.reciprocal(out=sm(11), in_=sm(11)), r=R, w=W)
        tt(DVE, sm(9), sm(9), sm(11), ALU.mult, r=R, w=W)
        tt(DVE, sm(10), sm(10), sm(11), ALU.mult, r=R, w=W)
        t0n, t1n, t2n, t3n = s5tn[0], s5tn[1], s5tn[2], s5tn[3]
        RT = ["s5sm", "cosT", "sinT"] + t0n + t1n; WT = ["cosT", "sinT"] + t0n + t1n
        cp(DVE, cosT[:, :, 0:1], s5sm[:, 5, :].unsqueeze(2), r=RT, w=WT)
        cp(DVE, sinT[:, :, 0:1], s5sm[:, 4, :].unsqueeze(2), r=RT, w=WT)
        m = 1
        while m < SL:
            br = cosT[:, :, m - 1:m].to_broadcast([128, 8, m]); bi = sinT[:, :, m - 1:m].to_broadcast([128, 8, m])
            a_ = s5t[0][:, 0:8 * m].rearrange("p (s t) -> p s t", s=8); b_ = s5t[1][:, 0:8 * m].rearrange("p (s t) -> p s t", s=8)
            tt(DVE, a_, cosT[:, :, 0:m], br, ALU.mult, r=RT, w=WT)
            tt(DVE, b_, sinT[:, :, 0:m], bi, ALU.mult, r=RT, w=WT)
            tt(DVE, cosT[:, :, m:2 * m], a_, b_, ALU.subtract, r=RT, w=WT)
            tt(DVE, a_, cosT[:, :, 0:m], bi, ALU.mult, r=RT, w=WT)
            tt(DVE, b_, sinT[:, :, 0:m], br, ALU.mult, r=RT, w=WT)
            tt(DVE, sinT[:, :, m:2 * m], a_, b_, ALU.add, r=RT, w=WT)
            m *= 2
        crb = s5sm[:, 9, :].unsqueeze(2).to_broadcast([128, 8, SL]); cib = s5sm[:, 10, :].unsqueeze(2).to_broadcast([128, 8, SL])
        v3 = lambda t: t[:, 0:8 * SL].rearrange("p (s t) -> p s t", s=8)
        RT2 = ["s5sm", "cosT", "sinT"]
        tt(DVE, v3(s5t[0]), cosT[:, :, :], crb, ALU.mult, r=RT2, w=t0n)
        tt(DVE, v3(s5t[1]), sinT[:, :, :], cib, ALU.mult, r=RT2, w=t1n)
        tt(DVE, T1r[:, :, :], v3(s5t[0]), v3(s5t[1]), ALU.add, r=t0n + t1n, w=["T1r"])
        tt(DVE, v3(s5t[2]), cosT[:, :, :], cib, ALU.mult, r=RT2, w=t2n)
        tt(DVE, v3(s5t[3]), sinT[:, :, :], crb, ALU.mult, r=RT2, w=t3n)
        tt(DVE, T1i[:, :, :], v3(s5t[2]), v3(s5t[3]), ALU.subtract, r=t2n + t3n, w=["T1i"])
        cp(DVE, Rm[:, :, :], s5sm[:, 3, :].unsqueeze(2).to_broadcast([128, 8, SL]), r=["s5sm"], w=["Rm"])
        S.op(DVE, lambda: V.memset(Rm[:, :, 0:1], 0.0), r=[], w=["Rm"])
        cp(DVE, RmS[:, :, :, :], Rm[:, :, 0:DEC_SEQ].unsqueeze(2).to_broadcast([128, 8, SPC, DEC_SEQ]), r=["Rm"], w=["RmS"])

    def sumsq_rstd(N, scale):
        bk, bn = pbank()
        for kt in range(8):
            sq = sqr[kt % 4]; sqn = "sqr%d" % (kt % 4)
            act(sq[:, 0:N], cur["x"][:, kt, 0:N], AF.Square, r=["%s_%d" % (cur["n"], kt)], w=[sqn])
            mm(bk[:, 0:N], ones_bf, sq[:, 0:N], kt == 0, kt == 7, r=["cbf", sqn], w=[bn])
        rsqrt_chain(rstd[:, 0:N], bk[:, 0:N], scale, r=[bn], w=["rstd"], tmpap=tmp[0][:, 0:N], tmpname="tmp0")

    def proj_and_norm(l, blk, N, tok0):
        src_ = d_xT if l == 0 else d_x1
        for kt in range(8):
            S.dma(cur["x"][:, kt, 0:N], src_[:, kt, tok0:tok0 + N], r=(["x1_%d_%d" % (blk, kt)] if l == 1 else []),
                  w=["%s_%d" % (cur["n"], kt)], chan="xin%d" % kt)
        S.dma(ropec[64:96, 0:N], d_rope[0, 64:96, tok0:tok0 + N], w=["ropec"], chan="rp0")
        S.dma(ropes[64:96, 0:N], d_rope[1, 64:96, tok0:tok0 + N], w=["ropes"], chan="rp1")
        lvl = cfg.get('proj_lvl', 9)
        if lvl < 2:
            return
        sumsq_rstd(N, 1.0 / D)
        if lvl < 3:
            return
        for kt in range(8):
            stt(DVE, h_bf[:, kt, 0:N], cur["x"][:, kt, 0:N], vecs[:, l * VL + kt:l * VL + kt + 1], rstd[:, 0:N], ALU.mult, ALU.mult,
                r=["%s_%d" % (cur["n"], kt), "rstd", "vecs"], w=["h_bf%d" % kt])
        if lvl < 4:
            return
        for t_ in range(NT_IN):
            if t_ in (2, 3):
                continue
            bk, bn = pbank()
            for kt in range(8):
                mm(bk[:, 0:N], win_bf[:, kt, t_ * 128:(t_ + 1) * 128], h_bf[:, kt, 0:N], kt == 0, kt == 7,
                   r=["win_bf%d" % kt, "h_bf%d" % kt], w=[bn])
            if t_ in (14, 15):
                cp(ACT, u_bf[:, t_ - 14, 0:N], bk[:, 0:N], r=[bn], w=["u_bf"])
            else:
                cp(ACT, pj[:, t_, 0:N], bk[:, 0:N], r=[bn], w=["pj%d" % t_])

    def gla_block(l, blk, N, tok0, L, sample):
        nch = N // L
        b = l * VL
        cp(DVE, glr_bf[:, 0:N], pj[0:16, 4, 0:N], r=["pj4"], w=["glr_bf"])
        bk, bn = pbank()
        mm(bk[:, 0:N], wgate_bf[:, :], glr_bf[:, 0:N], True, True, r=["wgate_bf", "glr_bf"], w=[bn])
        e_ = tmp[0][:, 0:N]; c_ = tmp[1][:, 0:N]; eb = tmp[2][:, 0:N]; enb = tmp[3][:, 0:N]
        act(e_, bk[:, 0:N], AF.Exp, r=[bn, "nvec"], w=["tmp0"], bias=nvec[:, 4 * l:4 * l + 1], scale=-1.0)
        act(e_, e_, AF.Ln, r=["tmp0"], w=["tmp0"], bias=1.0)
        gm = consts[:, C_GM64:C_GM64 + N] if not sample else consts[:, C_GM16:C_GM16 + N]
        S.op(DVE, lambda: V.tensor_tensor_scan(out=c_, data0=gm, data1=e_, initial=0.0, op0=ALU.mult, op1=ALU.add),
             r=["tmp0", "consts"], w=["tmp1"])
        yield
        act(eb, c_, AF.Exp, r=["tmp1"], w=["tmp2"], scale=-1.0 / 16)
        act(enb, c_, AF.Exp, r=["tmp1"], w=["tmp3"], scale=1.0 / 16)
        c3 = tmp[1][:, 0:N].rearrange("p (c l) -> p c l", l=L)
        act(dec[:, 0:nch], c3[:, :, L - 1], AF.Exp, r=["tmp1"], w=["dec"], scale=-1.0 / 16)
        stt(DVE, qt_bf[:, 0:N], pj[:, 0, 0:N], 32 ** -0.5, eb, ALU.mult, ALU.mult, r=["pj0", "tmp2"], w=["qt_bf"])
        tt(DVE, kt_bf[:, 0:N], pj[:, 1, 0:N], enb, ALU.mult, r=["pj1", "tmp3"], w=["kt_bf"])
        hm = consts[:, C_HM:C_HM + 4]
        tt(DVE, ktx_bf[:, :, 0:N], kt_bf[:, 0:N].unsqueeze(1).to_broadcast([128, 4, N]),
           hm.unsqueeze(2).to_broadcast([128, 4, N]), ALU.mult, r=["kt_bf", "consts"], w=["ktx_bf"])
        if cfg.get('gla_lvl', 9) < 2:
            return
        yield
        for c2 in range(nch // 2):
            bk, bn = pbank()
            for cc in range(2):
                c = 2 * c2 + cc
                for kt in range(8):
                    mm(bk[0:L, cc * 256:(cc + 1) * 256], h_bf[:, kt, c * L:(c + 1) * L], win_bf[:, kt, 256:512],
                       kt == 0, kt == 7, r=["win_bf%d" % kt, "h_bf%d" % kt], w=[bn])
            cp(ACT, vtok_bf[0:L, 2 * c2:2 * c2 + 2, :], bk[0:L, :].rearrange("p (a c) -> p a c", a=2), r=[bn], w=["vtok_bf"])
            yield
        bk, bn = pbank()
        bkb = bk[:].bitcast(BF16)
        for c in range(nch):
            S.op(PE, lambda c=c: T.transpose(bkb[0:L, c * 128:(c + 1) * 128], kt_bf[:, c * L:(c + 1) * L], ident_bf),
                 r=["kt_bf", "cbf"], w=[bn])
        cp(ACT, ktok_bf[0:L, 0:nch, :], bkb[0:L, 0:nch * 128].rearrange("p (a c) -> p a c", a=nch), r=[bn], w=["ktok_bf"])
        if cfg.get('gla_lvl', 9) < 3:
            return
        yield
        dsb, dsn = pbank()
        for c in range(nch):
            for h in range(4):
                mm(dsb[32 * h:32 * h + 32, c * 64:(c + 1) * 64], ktok_bf[0:L, c, 32 * h:32 * h + 32],
                   vtok_bf[0:L, c, 64 * h:64 * h + 64], True, True, r=["ktok_bf", "vtok_bf"], w=[dsn], tp=(0, 32 * h))
        yield
        if sample:
            S.dma(Sin[:, :, :], d_gla0[l].rearrange("c p v -> p c v"), w=["Sin"], chan="gl0")
        elif blk == 0:
            S.op(DVE if P2D else POOL, lambda: (V if P2D else G).memset(Sst[:], 0.0), r=[], w=["Sst"])
        for c in range(nch):
            yield
            if sample:
                tt(DVE, Sall_bf[:, c, :, :], Sin[:, c, :].unsqueeze(1).to_broadcast([128, 4, 64]),
                   hm.unsqueeze(2).to_broadcast([128, 4, 64]), ALU.mult, r=["Sin", "consts"], w=["Sall_bf"])
                tt(DVE, Stmp[:], Sin[:, c, :], dsb[:, c * 64:(c + 1) * 64], ALU.add, r=["Sin", dsn], w=["Stmp"])
                ts(DVE, Stmp[:], Stmp[:], dec[:, c:c + 1], None, ALU.mult, None, r=["Stmp", "dec"], w=["Stmp"])
                S.dma(o_gla[l, 1 + c], Stmp[:], r=["Stmp"], chan="go")
            else:
                tt(DVE, Sall_bf[:, c, :, :], Sst[:].unsqueeze(1).to_broadcast([128, 4, 64]),
                   hm.unsqueeze(2).to_broadcast([128, 4, 64]), ALU.mult, r=["Sst", "consts"], w=["Sall_bf"])
                tt(DVE, Stmp[:], Sst[:], dsb[:, c * 64:(c + 1) * 64], ALU.add, r=["Sst", dsn], w=["Stmp"])
                ts(DVE, Sst[:], Stmp[:], dec[:, c:c + 1], None, ALU.mult, None, r=["Stmp", "dec"], w=["Sst"])
        if (not sample) and blk == NPB - 1:
            S.dma(o_gla[l, 0], Sst[:], r=["Sst"], chan="go")
        if cfg.get('gla_lvl', 9) < 4:
            return
        yield
        cpb = 512 // (4 * L)
        cm = consts[0:L, C_CM64:C_CM64 + L] if L == 64 else consts[0:L, C_CM16:C_CM16 + L]
        for g in range(nch // cpb if nch >= cpb else 1):
            ncg = min(cpb, nch)
            bk, bn = pbank()
            for cc in range(ncg):
                c = g * cpb + cc
                for h in range(4):
                    mm(bk[0:L, (cc * 4 + h) * L:(cc * 4 + h + 1) * L], ktx_bf[:, h, c * L:(c + 1) * L],
                       qt_bf[:, c * L:(c + 1) * L], True, True, r=["ktx_bf", "qt_bf"], w=[bn])
            tt(DVE, scT_bf[0:L, g * cpb:g * cpb + ncg, :, 0:L],
               bk[0:L, 0:ncg * 4 * L].rearrange("p (a h i) -> p a h i", a=ncg, h=4),
               cm.unsqueeze(1).unsqueeze(1).to_broadcast([L, ncg, 4, L]), ALU.mult, r=[bn, "consts"], w=["scT_bf"])
        if cfg.get('gla_lvl', 9) < 5:
            return
        yield
        obk = [(banks[4], "bank4"), (banks[5], "bank5")]
        for c in range(nch):
            yield
            for h in range(4):
                ob, on = obk[h // 2]
                po = 64 * (h % 2)
                mm(ob[po:po + 64, c * L:(c + 1) * L], vtok_bf[0:L, c, 64 * h:64 * h + 64], scT_bf[0:L, c, h, 0:L],
                   True, False, r=["vtok_bf", "scT_bf"], w=[on], tp=(0, po))
                mm(ob[po:po + 64, c * L:(c + 1) * L], Sall_bf[:, c, h, :], qt_bf[:, c * L:(c + 1) * L],
                   False, True, r=["Sall_bf", "qt_bf"], w=[on], tp=(0, po))
        if cfg.get('gla_lvl', 9) < 6:
            return
        yield
        for hp in range(2):
            ob, on = obk[hp]
            act(tb[0][:, hp, 0:N], ob[:, 0:N], AF.Square, r=[on], w=["tb0"])
        sgz = big[0]
        sigmoid_chain(sgz[:, 0:2, 0:N], pj[:, 5:7, 0:N], r=["pj5", "pj6"], w=["big0a"], t1=big[1][:, 0:2, 0:N], t1n="big1a")
        tt(DVE, sgz[:, 0:2, 0:N], pj[:, 5:7, 0:N], sgz[:, 0:2, 0:N], ALU.mult, r=["pj5", "pj6", "big0a"], w=["big0a"])
        for hp in range(2):
            yield
            ob, on = obk[hp]
            bk, bn = pbank()
            mm(bk[:, 0:N], blk_bf, tb[0][:, hp, 0:N], True, True, r=["cbf", "tb0"], w=[bn])
            rsqrt_chain(tmp[4][:, 0:N], bk[:, 0:N], 1.0 / 64, r=[bn], w=["tmp4"], tmpap=tmp[5][:, 0:N], tmpname="tmp5")
            stt(DVE, tmp[6][:, 0:N], ob[:, 0:N], vecs[:, b + 12:b + 13], tmp[4][:, 0:N], ALU.mult, ALU.mult,
                r=[on, "vecs", "tmp4"], w=["tmp6"])
            tt(DVE, ocat_bf[:, hp, 0:N], tmp[6][:, 0:N], sgz[:, hp, 0:N], ALU.mult, r=["tmp6", "big0a"], w=["ocat%d" % hp])

    def mla_qkv(l, blk, N, tok0, sample):
        b = l * VL
        act(tb[1][:, 0, 0:N], pj[:, 7, 0:N], AF.Square, r=["pj7"], w=["tb1"])
        act(tb[1][0:64, 1, 0:N], pj[0:64, 8, 0:N], AF.Square, r=["pj8"], w=["tb1"])
        bk, bn = pbank2()
        mm(bk[:, 0:N], ones_bf, tb[1][:, 0, 0:N], True, False, r=["cbf", "tb1"], w=[bn])
        mm(bk[:, 0:N], ones_bf[0:64, :], tb[1][0:64, 1, 0:N], False, True, r=["cbf", "tb1"], w=[bn])
        rsqrt_chain(big[0][:, 2, 0:N], bk[:, 0:N], 1.0 / 192, r=[bn], w=["big0b"], tmpap=big[0][:, 3, 0:N], tmpname="big0b")
        stt(DVE, cqn_bf[:, 0, 0:N], pj[:, 7, 0:N], vecs[:, b + 8:b + 9], big[0][:, 2, 0:N], ALU.mult, ALU.mult,
            r=["pj7", "big0b", "vecs"], w=["cqn_bf"])
        stt(DVE, cqn_bf[0:64, 1, 0:N], pj[0:64, 8, 0:N], vecs[0:64, b + 9:b + 10], big[0][0:64, 2, 0:N], ALU.mult, ALU.mult,
            r=["pj8", "big0b", "vecs"], w=["cqn_bf"])
        yield
        for h in range(4):
            yield
            bkA, bnA = pbank2()
            bkB, bnB = pbank2()
            for (bk_, bn_, off) in ((bkA, bnA, 2 * h * 96), (bkB, bnB, (2 * h + 1) * 96)):
                mm(bk_[0:96, 0:N], wuq_bf[:, 0, off:off + 96], cqn_bf[:, 0, 0:N], True, False, r=["wuq_bf", "cqn_bf"], w=[bn_])
                mm(bk_[0:96, 0:N], wuq_bf[0:64, 1, off:off + 96], cqn_bf[0:64, 1, 0:N], False, True, r=["wuq_bf", "cqn_bf"], w=[bn_])
            cp(ACT, Q_bf[0:64, h, 0:N], bkA[0:64, 0:N], r=[bnA], w=["Q%d" % h])
            tt(DVE, big[1][64:96, 2, 0:N], bkA[64:96, 0:N], ropec[64:96, 0:N], ALU.mult, r=[bnA, "ropec"], w=["big1b"])
            tt(DVE, big[1][64:96, 3, 0:N], bkB[64:96, 0:N], ropes[64:96, 0:N], ALU.mult, r=[bnB, "ropes"], w=["big1b"])
            tt(POOL, Q_bf[64:96, h, 0:N], big[1][64:96, 2, 0:N], big[1][64:96, 3, 0:N], ALU.add, r=["big1b", "big1b"], w=["Q%d" % h])
        yield
        tt(DVE, big[1][64:96, 2, 0:N], pj[64:96, 8, 0:N], ropec[64:96, 0:N], ALU.mult, r=["pj8", "ropec"], w=["big1b"])
        tt(DVE, big[1][64:96, 3, 0:N], pj[64:96, 4, 0:N], ropes[64:96, 0:N], ALU.mult, r=["pj4", "ropes"], w=["big1b"])
        tt(POOL, kpe_f[64:96, 0:N], big[1][64:96, 2, 0:N], big[1][64:96, 3, 0:N], ALU.add, r=["big1b", "big1b"], w=["kpe_f"])
        S.dma(o_kpe[l, :, tok0:tok0 + N], kpe_f[64:96, 0:N], r=["kpe_f"], chan="ko")
        yield
        act(tb[1][:, 0, 0:N], pj[:, 9, 0:N], AF.Square, r=["pj9"], w=["tb1"])
        bk, bn = pbank2()
        mm(bk[:, 0:N], ones_bf, tb[1][:, 0, 0:N], True, True, r=["cbf", "tb1"], w=[bn])
        rsqrt_chain(big[0][:, 2, 0:N], bk[:, 0:N], 1.0 / 128, r=[bn], w=["big0b"], tmpap=big[0][:, 3, 0:N], tmpname="big0b")
        stt(DVE, ckv_f[:, 0:N], pj[:, 9, 0:N], vecs[:, b + 10:b + 11], big[0][:, 2, 0:N], ALU.mult, ALU.mult,
            r=["pj9", "vecs", "big0b"], w=["ckv_f"])
        S.dma(o_ckv[l, :, tok0:tok0 + N], ckv_f[:, 0:N], r=["ckv_f"], chan="co")
        cp(ACT, ckv_bf[:, 0:N], ckv_f[:, 0:N], r=["ckv_f"], w=["ckv_bf"])

    def kres(h, j):
        return "K%d_%d" % (h, j)

    def attn_tail(l, N, Oaps, sumaps, onames, snames):
        for h in range(4):
            rc = tmp[4][:, 0:N]
            act(tmp[5][:, 0:N], sumaps[h], AF.Ln, r=[snames[h]], w=["tmp5"])
            act(rc, tmp[5][:, 0:N], AF.Exp, r=["tmp5"], w=["tmp4"], scale=-1.0)
            tt(DVE, tmp[6][:, 0:N], Oaps[h], rc, ALU.mult, r=[onames[h], "tmp4"], w=["tmp6"])
            tt(POOL, tmp[7][:, 0:N], pj[:, 10 + h, 0:N], big[0][:, h, 0:N], ALU.mult, r=["pj%d" % (10 + h)] + BIG0, w=["tmp7"])
            tt(DVE, ocat_bf[:, 2 + h, 0:N], tmp[6][:, 0:N], tmp[7][:, 0:N], ALU.mult, r=["tmp6", "tmp7"], w=["ocat%d" % (2 + h)])

    def mla_prompt_pre(l, blk, N, tok0):
        for h in range(4):
            bk, bn = pbank()
            mm(bk[0:64, 0:N], wukv_bf[:, 64 * h:64 * h + 64], ckv_bf[:, 0:N], True, True, r=["wukv_bf", "ckv_bf"], w=[bn])
            wr = [kres(h, 2 * blk), kres(h, 2 * blk + 1)]
            cp(ACT, Kbuf[0:64, h, tok0:tok0 + N], bk[0:64, 0:N], r=[bn], w=wr)
            cp(ACT, Kbuf[64:96, h, tok0:tok0 + N], kpe_f[64:96, 0:N], r=["kpe_f"], w=wr)
        for a in range(2):
            j = 2 * blk + a
            bk, bn = pbank()
            mm(bk[:, :], ckv_bf[:, a * 128:(a + 1) * 128], wukv_bf[:, 256:768], True, True, r=["ckv_bf", "wukv_bf"], w=[bn])
            cp(ACT, Vbuf[:, j, :], bk[:, :], r=[bn], w=["V%d" % j])
        sigmoid_chain(big[0][:, 0:4, 0:N], pj[:, 10:14, 0:N], r=["pj10", "pj11", "pj12", "pj13"], w=BIG0,
                      t1=big[1][:, 0:4, 0:N], t1n=BIG1)
        tt(DVE, big[0][:, 0:4, 0:N], pj[:, 10:14, 0:N], big[0][:, 0:4, 0:N], ALU.mult,
           r=["pj10", "pj11", "pj12", "pj13"] + BIG0, w=BIG0)

    def mla_prompt_gen(l, blk, N, tok0):
        scale = 96 ** -0.5
        pti = 0
        njt = 2 * blk + 2
        items = [(h, j) for h in range(4) for j in range(njt)]

        def emit_scores(h, j):
            c0 = 0 if j <= 2 * blk else 128
            bk, bn = pbank()
            mm(bk[:, c0:N], Kbuf[0:96, h, j * 128:(j + 1) * 128], Q_bf[0:96, h, c0:N], True, True,
               r=[kres(h, j), "Q%d" % h], w=[bn])
            return bk, bn
        nxt = emit_scores(*items[0])
        for idx, (h, j) in enumerate(items):
            Ob, On = banks[4 + (h % 2)], "bank%d" % (4 + (h % 2))
            Sb, Sn = banks[6], "bank6a"
            if cfg.get('wi_heads'):
                On = "wiO%d" % h; Sn = "wiS%d" % h
            bk, bn = nxt
            if idx + 1 < len(items):
                nxt = emit_scores(*items[idx + 1])
            c0 = 0 if j <= 2 * blk else 128
            npt = cfg.get('wi_npt', 3)
            p_ = pT[pti % 3]; pn = "pT%d" % (pti % npt); pti += 1
            if j < 2 * blk:
                act(p_[:, 0:N], bk[:, 0:N], AF.Exp, r=[bn], w=[pn], scale=scale)
            else:
                a = j - 2 * blk
                d0 = a * 128
                act(p_[0:64, d0:d0 + 128], bk[0:64, d0:d0 + 128], AF.Exp, r=[bn], w=[pn], scale=scale)
                act(p_[64:128, d0 + 64:d0 + 128], bk[64:128, d0 + 64:d0 + 128], AF.Exp, r=[bn], w=[pn], scale=scale)
                S.op(DVE if P2D else POOL, lambda p_=p_, d0=d0: (V if P2D else G).memset(p_[64:128, d0:d0 + 64], 0.0), r=[], w=[pn])
                if a == 0:
                    act(p_[:, 128:N], bk[:, 128:N], AF.Exp, r=[bn], w=[pn], scale=scale)
            st = (j == 0)
            mm(Ob[:, c0:N], Vbuf[:, j, 128 * h:128 * h + 128], p_[:, c0:N], st, j == njt - 1, r=["V%d" % j, pn], w=[On])
            mm(Sb[:, c0:N], ones_bf, p_[:, c0:N], st, j == njt - 1, r=["cbf", pn], w=[Sn])
            yield
            if j == njt - 1:
                attn_tail_one(l, N, h, Ob[:, 0:N], Sb[:, 0:N], On, Sn)
                yield

    def attn_tail_one(l, N, h, Oap, sap, on, sn):
        if cfg.get('wi_tail'):
            sfx = "_wt%d" % h
            act(tmp[5][:, 0:N], sap, AF.Ln, r=[sn], w=["tmp5" + sfx])
            act(tmp[4][:, 0:N], tmp[5][:, 0:N], AF.Exp, r=["tmp5" + sfx], w=["tmp4" + sfx], scale=-1.0)
            tt(DVE, tmp[6][:, 0:N], Oap, tmp[4][:, 0:N], ALU.mult, r=[on, "tmp4" + sfx], w=["tmp6" + sfx])
            tt(POOL, tmp[7][:, 0:N], pj[:, 10 + h, 0:N], big[0][:, h, 0:N], ALU.mult, r=["pj%d" % (10 + h)] + BIG0, w=["tmp7" + sfx])
            tt(DVE, ocat_bf[:, 2 + h, 0:N], tmp[6][:, 0:N], tmp[7][:, 0:N], ALU.mult, r=["tmp6" + sfx, "tmp7" + sfx], w=["ocat%d" % (2 + h)])
            return
        rc = tmp[4][:, 0:N]
        act(tmp[5][:, 0:N], sap, AF.Ln, r=[sn], w=["tmp5"])
        act(rc, tmp[5][:, 0:N], AF.Exp, r=["tmp5"], w=["tmp4"], scale=-1.0)
        tt(DVE, tmp[6][:, 0:N], Oap, rc, ALU.mult, r=[on, "tmp4"], w=["tmp6"])
        tt(DVE, ocat_bf[:, 2 + h, 0:N], tmp[6][:, 0:N], big[0][:, h, 0:N], ALU.mult, r=["tmp6"] + BIG0, w=["ocat%d" % (2 + h)])

    def mla_sample(l, N):
        scale = 96 ** -0.5
        NK = PAST + DEC_SEQ
        sigmoid_chain(big[0][:, 0:4, 0:N], pj[:, 10:14, 0:N], r=["pj10", "pj11", "pj12", "pj13"], w=BIG0,
                      t1=big[1][:, 0:4, 0:N], t1n=BIG1)
        tt(DVE, big[0][:, 0:4, 0:N], pj[:, 10:14, 0:N], big[0][:, 0:4, 0:N], ALU.mult,
           r=["pj10", "pj11", "pj12", "pj13"] + BIG0, w=BIG0)
        Ob, On = banks[4], "bank4"
        Sb, Sn = banks[6], "bank6a"
        allK = [kres(h, j) for h in range(4) for j in range(9)]
        allV = ["V%d" % j for j in range(9)]
        for q in range(SPC):
            cs = slice(q * DEC_SEQ, (q + 1) * DEC_SEQ)
            S.dma(ckvall_bf[:, 0:PAST], d_ckvP[l, q], w=["ckvall_bf"], chan="cp0", queue="pool")
            cp(DVE, ckvall_bf[:, PAST:NK], ckv_bf[:, cs], r=["ckv_bf"], w=["ckvall_bf"])
            for h in range(4):
                S.dma(Kbuf[64:96, h, 0:PAST], d_kpeP[l, q], w=[kres(h, j) for j in range(8)], chan="cp%d" % (1 + h), queue="pool")
                cp(POOL, Kbuf[64:96, h, PAST:NK], kpe_f[64:96, cs], r=["kpe_f"], w=[kres(h, 8)])
                for (c0, c1) in ((0, 512), (512, 1024), (1024, NK)):
                    bk, bn = pbank()
                    mm(bk[0:64, 0:c1 - c0], wukv_bf[:, 64 * h:64 * h + 64], ckvall_bf[:, c0:c1], True, True,
                       r=["wukv_bf", "ckvall_bf"], w=[bn])
                    cp(ACT if h % 2 == 0 else DVE, Kbuf[0:64, h, c0:c1], bk[0:64, 0:c1 - c0], r=[bn],
                       w=[kres(h, j) for j in range(c0 // 128, (c1 + 127) // 128)])
            for j in range(9):
                nk = 128 if j < 8 else DEC_SEQ
                bk, bn = pbank()
                mm(bk[0:nk, :], ckvall_bf[:, j * 128:j * 128 + nk], wukv_bf[:, 256:768], True, True, r=["ckvall_bf", "wukv_bf"], w=[bn])
                cp(ACT if j % 2 == 0 else DVE, Vbuf[0:nk, j, :], bk[0:nk, :], r=[bn], w=["V%d" % j])
            for h in range(4):
                bk, bn = pbank()
                for j in range(9):
                    nk = 128 if j < 8 else DEC_SEQ
                    mm(bk[0:nk, j * 16:(j + 1) * 16], Kbuf[0:96, h, j * 128:j * 128 + nk], Q_bf[0:96, h, cs], True, True,
                       r=[kres(h, j), "Q%d" % h], w=[bn])
                p_ = pT[(q * 4 + h) % 3]; pn = "pT%d" % ((q * 4 + h) % 3)
                act(p_[:, 0:128], bk[:, 0:128], AF.Exp, r=[bn], w=[pn], scale=scale)
                act(p_[0:16, 128:144], bk[0:16, 128:144], AF.Exp, r=[bn], w=[pn], scale=scale)
                oc = slice(h * NS + q * DEC_SEQ, h * NS + (q + 1) * DEC_SEQ)
                for j in range(9):
                    nk = 128 if j < 8 else DEC_SEQ
                    mm(Ob[:, oc], Vbuf[0:nk, j, 128 * h:128 * h + 128], p_[0:nk, j * 16:(j + 1) * 16], j == 0, j == 8,
                       r=["V%d" % j, pn], w=[On])
                    mm(Sb[:, oc], ones_bf[0:nk, :], p_[0:nk, j * 16:(j + 1) * 16], j == 0, j == 8, r=["cbf", pn], w=[Sn, "bank6b"])
        for h in range(4):
            attn_tail_one(l, N, h, Ob[:, h * NS:(h + 1) * NS], Sb[:, h * NS:(h + 1) * NS], On, Sn)

    def s5_pre(l, blk, N, tok0, sample):
        if sample:
            S.dma(xpr[:, :, :], d_s5x0[l, 0].rearrange("q p s -> p q s"), w=["xpr"], chan="sx0")
            S.dma(xpi[:, :, :], d_s5x0[l, 1].rearrange("q p s -> p q s"), w=["xpi"], chan="sx1")
        elif blk == 0:
            S.op(DVE if P2D else POOL, lambda: (V if P2D else G).memset(xpr[:], 0.0), r=[], w=["xpr"])
            S.op(DVE if P2D else POOL, lambda: (V if P2D else G).memset(xpi[:], 0.0), r=[], w=["xpi"])

    def s5_chunks_gen(l, blk, N, tok0, sample):
        nseq = SPC if sample else 1
        L = DEC_SEQ if sample else SL
        W = nseq * L
        nchunk = N // W
        v4 = lambda ap: ap.rearrange("p (s q t) -> p s q t", s=8, q=nseq)
        tb4 = lambda tab: tab[:, :, 0:L].unsqueeze(2).to_broadcast([128, 8, nseq, L])
        T4 = [v4(sx[i][:, 0:8 * W]) for i in range(8)]
        F2 = [sx[i][:, 0:8 * W] for i in range(8)]
        xprv = xpr[:, 0:nseq, :].rearrange("p q s -> p s q"); xpiv = xpi[:, 0:nseq, :].rearrange("p q s -> p s q")
        rm2 = (RmS[:, :, :, :].rearrange("p s q t -> p (s q t)") if sample else Rm[:, :, :].rearrange("p s t -> p (s t)"))
        rmn = "RmS" if sample else "Rm"
        br, brn = banks[2], "bank2"
        bi, bin_ = banks[3], "bank3"
        br4 = v4(br[:, 0:8 * W]); bi4 = v4(bi[:, 0:8 * W])
        X, Y, Xn, Yn = 6, 7, sxn[6], sxn[7]
        V0, V1 = 4, 5

        def front_ops(c):
            c0 = c * W
            z0, z1 = 2 * (c % 2), 2 * (c % 2) + 1
            ops = []

            def bu():
                for s in range(8):
                    mm(br[:, s * W:(s + 1) * W], wB_bf[:, 0, s, :], u_bf[:, s // 4, c0:c0 + W], True, True, r=["wB_bf", "u_bf"], w=[brn])
                    mm(bi[:, s * W:(s + 1) * W], wB_bf[:, 1, s, :], u_bf[:, s // 4, c0:c0 + W], True, True, r=["wB_bf", "u_bf"], w=[bin_])
            ops.append(bu)
            ops.append(lambda: tt(DVE, T4[X], br4, tb4(T1r), ALU.mult, r=[brn, "T1r"], w=Xn))
            ops.append(lambda: tt(DVE, T4[Y], bi4, tb4(T1i), ALU.mult, r=[bin_, "T1i"], w=Yn))
            ops.append(lambda: tt(POOL, T4[z0], T4[X], T4[Y], ALU.subtract, r=Xn + Yn, w=sxn[z0]))
            ops.append(lambda: tt(DVE, T4[X], br4, tb4(T1i), ALU.mult, r=[brn, "T1i"], w=Xn))
            ops.append(lambda: tt(DVE, T4[Y], bi4, tb4(T1r), ALU.mult, r=[bin_, "T1r"], w=Yn))
            ops.append(lambda: tt(POOL, T4[z1], T4[X], T4[Y], ALU.add, r=Xn + Yn, w=sxn[z1]))
            return ops

        def back_ops(c):
            c0 = c * W
            z0, z1 = 2 * (c % 2), 2 * (c % 2) + 1
            rb = s5sm[:, 3, :].unsqueeze(2).to_broadcast([128, 8, nseq])
            xr4 = xr_bf[:, :, 0:W].rearrange("p s (q t) -> p s q t", q=nseq)
            xi4 = xi_bf[:, :, 0:W].rearrange("p s (q t) -> p s q t", q=nseq)
            ops = []
            ops.append(lambda: tt(POOL, rxr[:, :, 0:nseq], xprv, rb, ALU.mult, r=["xpr", "s5sm"], w=["rxr"]))
            ops.append(lambda: tt(POOL, T4[z0][:, :, :, 0], T4[z0][:, :, :, 0], rxr[:, :, 0:nseq], ALU.add, r=sxn[z0] + ["rxr"], w=sxn[z0]))
            ops.append(lambda: S.op(DVE, lambda: V.tensor_tensor_scan(out=F2[V0], data0=rm2, data1=F2[z0], initial=0.0, op0=ALU.mult, op1=ALU.add),
                                    r=sxn[z0] + [rmn], w=sxn[V0], cost=1200.0))
            ops.append(lambda: tt(POOL, rxi[:, :, 0:nseq], xpiv, rb, ALU.mult, r=["xpi", "s5sm"], w=["rxi"]))
            ops.append(lambda: tt(POOL, T4[z1][:, :, :, 0], T4[z1][:, :, :, 0], rxi[:, :, 0:nseq], ALU.add, r=sxn[z1] + ["rxi"], w=sxn[z1]))
            ops.append(lambda: S.op(DVE, lambda: V.tensor_tensor_scan(out=F2[V1], data0=rm2, data1=F2[z1], initial=0.0, op0=ALU.mult, op1=ALU.add),
                                    r=sxn[z1] + [rmn], w=sxn[V1], cost=1200.0))
            ops.append(lambda: tt(DVE, T4[z0], T4[V0], tb4(cosT), ALU.mult, r=sxn[V0] + ["cosT"], w=sxn[z0]))
            ops.append(lambda: tt(POOL, T4[z1], T4[V1], tb4(sinT), ALU.mult, r=sxn[V1] + ["sinT"], w=sxn[z1]))
            ops.append(lambda: tt(POOL, xprv, T4[z0][:, :, :, L - 1], T4[z1][:, :, :, L - 1], ALU.subtract, r=sxn[z0] + sxn[z1], w=["xpr"]))
            ops.append(lambda: tt(DVE, xr4, T4[z0], T4[z1], ALU.subtract, r=sxn[z0] + sxn[z1], w=["xr_bf"]))
            ops.append(lambda: tt(DVE, T4[z0], T4[V0], tb4(sinT), ALU.mult, r=sxn[V0] + ["sinT"], w=sxn[z0]))
            ops.append(lambda: tt(POOL, T4[z1], T4[V1], tb4(cosT), ALU.mult, r=sxn[V1] + ["cosT"], w=sxn[z1]))
            ops.append(lambda: tt(POOL, xpiv, T4[z0][:, :, :, L - 1], T4[z1][:, :, :, L - 1], ALU.add, r=sxn[z0] + sxn[z1], w=["xpi"]))
            ops.append(lambda: stt(DVE, xi4, T4[z0], -1.0, T4[z1], ALU.mult, ALU.subtract, r=sxn[z0] + sxn[z1], w=["xi_bf"]))

            def ymm():
                yb_, ybn_ = pbank()
                for uf in range(2):
                    yo = yb_[:, uf * W:(uf + 1) * W]
                    for s in range(4):
                        sg = 4 * uf + s
                        mm(yo, wC_bf[:, 0, sg, :], xr_bf[:, sg, 0:W], s == 0, False, r=["wC_bf", "xr_bf"], w=[ybn_])
                        mm(yo, wC_bf[:, 1, sg, :], xi_bf[:, sg, 0:W], False, False, r=["wC_bf", "xi_bf"], w=[ybn_])
                    mm(yo, dD_bf[:, uf, :], u_bf[:, uf, c0:c0 + W], False, True, r=["dD_bf", "u_bf"], w=[ybn_])
                cp(ACT, y_sb[:, :, c0:c0 + W], yb_[:, 0:2 * W].rearrange("p (a n) -> p a n", a=2), r=[ybn_], w=["ckvall_bf"])
            ops.append(ymm)
            return ops

        for o_ in front_ops(0):
            o_()
            yield
        for c in range(nchunk):
            fo = front_ops(c + 1) if c + 1 < nchunk else []
            bo = back_ops(c)
            i = j = 0
            while i < len(fo) or j < len(bo):
                for _ in range(2):
                    if j < len(bo):
                        bo[j](); j += 1
                if i < len(fo):
                    fo[i](); i += 1
                yield

    def s5_tail(l, blk, N, tok0, sample):
        ybn = "ckvall_bf"
        if sample:
            S.dma(o_s5[l, 0, 1:1 + SPC].rearrange("q p s -> p q s"), xpr[:, :, :], r=["xpr"], chan="so0")
            S.dma(o_s5[l, 1, 1:1 + SPC].rearrange("q p s -> p q s"), xpi[:, :, :], r=["xpi"], chan="so1")
        elif blk == NPB - 1:
            S.dma(o_s5[l, 0, 0], xpr[:, 0, :], r=["xpr"], chan="so0")
            S.dma(o_s5[l, 1, 0], xpi[:, 0, :], r=["xpi"], chan="so1")
        y2 = y_sb[:, :, 0:N]
        sq_ = big[1][:, 0:2, 0:N]; u_ = big[1][:, 2:4, 0:N]
        act(sq_, y2, AF.Square, r=[ybn], w=BIG1)
        act(sq_, sq_, AF.Identity, r=BIG1, w=BIG1, bias=1.0, scale=0.044715)
        tt(DVE, u_, sq_, y2, ALU.mult, r=BIG1 + [ybn], w=BIG1)
        ts(DVE, u_, u_, 20.0, -20.0, ALU.min, ALU.max, r=BIG1, w=BIG1)
        sigmoid_chain(u_, u_, r=BIG1, w=BIG1, t1=sq_, t1n=BIG1, scale=2.0 * GELU_C)
        tt(DVE, g5[:, :, 0:N], y2, u_, ALU.mult, r=BIG1 + [ybn], w=["g5"])
        cp(ACT, g5_bf[:, :, 0:N], g5[:, :, 0:N], r=["g5"], w=["g5_bf"])
        sigmoid_chain(big[0][:, 0:2, 0:N], pj[:, 16:18, 0:N], r=["pj16", "pj17"], w=["big0a"], t1=big[1][:, 0:2, 0:N], t1n=BIG1)
        tt(DVE, big[0][:, 0:2, 0:N], pj[:, 16:18, 0:N], big[0][:, 0:2, 0:N], ALU.mult, r=["pj16", "pj17", "big0a"], w=["big0a"])
        for ot in range(2):
            bk, bn = pbank()
            for kt in range(2):
                mm(bk[:, 0:N], wglu_bf[:, kt, ot * 128:(ot + 1) * 128], g5_bf[:, kt, 0:N], kt == 0, kt == 1, r=["wglu_bf", "g5_bf"], w=[bn])
            sigmoid_chain(tmp[4][:, 0:N], bk[:, 0:N], r=[bn, "nvec"], w=["tmp4"], t1=tmp[5][:, 0:N], t1n="tmp5",
                          nbias=nvec[:, 4 * l + 1 + ot:4 * l + 2 + ot])
            tt(DVE, tmp[6][:, 0:N], g5[:, ot, 0:N], tmp[4][:, 0:N], ALU.mult, r=["g5", "tmp4"], w=["tmp6"])
            tt(DVE, ocat_bf[:, 6 + ot, 0:N], tmp[6][:, 0:N], big[0][:, ot, 0:N], ALU.mult, r=["tmp6", "big0a"], w=["ocat%d" % (6 + ot)])

    def out_block(l, blk, N, tok0):
        oc = ["ocat%d" % i for i in range(8)]
        for (k0, k1) in ((0, 6), (6, 8)):
            for ot in range(8):
                bk, bn = pbank()
                for kt in range(k0, k1):
                    mm(bk[:, 0:N], wout_bf[:, kt, ot * 128:(ot + 1) * 128], ocat_bf[:, kt, 0:N], kt == k0, kt == k1 - 1,
                       r=["wout_bf%d" % kt, "ocat%d" % kt], w=[bn])
                xn = "%s_%d" % (cur["n"], ot)
                tt(DVE, cur["x"][:, ot, 0:N], cur["x"][:, ot, 0:N], bk[:, 0:N], ALU.add, r=[xn, bn], w=[xn])
                if l == 0 and k1 == 8:
                    S.dma(d_x1[:, ot, tok0:tok0 + N], cur["x"][:, ot, 0:N], r=[xn], w=["x1_%d_%d" % (blk, ot)], chan="x1o%d" % ot)
        if l == 0:
            pass
        else:
            sumsq_rstd(N, 1.0 / D)
            yn = ["pj%d" % i for i in range(8)]
            for kt in range(8):
                stt(DVE, ybuf[:, kt, 0:N], cur["x"][:, kt, 0:N], vecs[:, 2 * VL + kt:2 * VL + kt + 1], rstd[:, 0:N],
                    ALU.mult, ALU.mult, r=["%s_%d" % (cur["n"], kt), "vecs", "rstd"], w=[yn[kt]])
                S.dma(o_yT[:, kt, tok0:tok0 + N], ybuf[:, kt, 0:N], r=[yn[kt]], chan="yo%d" % kt)

    for l in range(c_layers):
        if c_setup:
            load_weights(l)
            s5_setup(l)
        for blk in c_blocks:
            sample = blk == NPB
            cur["x"] = xblks[blk % 2]; cur["n"] = "xblk%d" % (blk % 2)
            N = NS if sample else NB
            tok0 = blk * NB
            if 'proj' in c_st:
                proj_and_norm(l, blk, N, tok0)
            ga = gla_block(l, blk, N, tok0, DEC_SEQ if sample else 64, sample) if 'gla' in c_st else iter(())
            gq = mla_qkv(l, blk, N, tok0, sample) if 'qkv' in c_st else iter(())
            if cfg.get('seq'):
                for _ in ga:
                    pass
                for _ in gq:
                    pass
            else:
                da = dq = False
                while not (da and dq):
                    if not da:
                        try:
                            next(ga)
                        except StopIteration:
                            da = True
                    if not dq:
                        try:
                            next(gq)
                        except StopIteration:
                            dq = True
            if 'mla' in c_st and 's5' in c_st and not sample and not cfg.get('seq'):
                mla_prompt_pre(l, blk, N, tok0)
                s5_pre(l, blk, N, tok0, sample)
                g1 = mla_prompt_gen(l, blk, N, tok0)
                def _coarse(g, k):
                    i = 0
                    for _ in g:
                        i += 1
                        if i % k == 0:
                            yield
                kk = cfg.get('ilv_k', 1)
                g2 = _coarse(s5_chunks_gen(l, blk, N, tok0, sample), kk)
                n1 = 4 * (2 * blk + 3); n2 = ((N // 64) * 8 + 7) // kk
                done1 = done2 = False
                k1 = k2 = 0
                while not (done1 and done2):
                    if not done1 and (done2 or k1 * n2 <= k2 * n1):
                        try:
                            next(g1); k1 += 1
                        except StopIteration:
                            done1 = True
                    elif not done2:
                        try:
                            next(g2); k2 += 1
                        except StopIteration:
                            done2 = True
                s5_tail(l, blk, N, tok0, sample)
            else:
                if 'mla' in c_st:
                    if sample:
                        mla_sample(l, N)
                    else:
                        mla_prompt_pre(l, blk, N, tok0)
                        for _ in mla_prompt_gen(l, blk, N, tok0):
                            pass
                if 's5' in c_st:
                    s5_pre(l, blk, N, tok0, sample)
                    for _ in s5_chunks_gen(l, blk, N, tok0, sample):
                        pass
                    s5_tail(l, blk, N, tok0, sample)
            if 'out' in c_st:
                out_block(l, blk, N, tok0)
    if cfg.get('sched', True):
        mk = S.schedule()
    else:
        mk = 0.0
    stats = S.emit()
    stats['sim_us'] = mk / 1e3
    return nc, es, stats


def _win_colmap():
    m = -np.ones(WIN_COLS, dtype=np.int64)

    def put(tile, c0, src):
        src = np.asarray(list(src))
        m[tile * 128 + c0: tile * 128 + c0 + len(src)] = src
    put(0, 0, range(0, 128)); put(1, 0, range(128, 256))
    put(2, 0, range(256, 384)); put(3, 0, range(384, 512))
    put(4, 0, range(512, 528)); put(4, 64, [1104 + (r + 16) % 32 for r in range(32)])
    put(5, 0, range(528, 656)); put(6, 0, range(656, 784))
    put(7, 0, range(784, 912)); put(8, 0, range(912, 976)); put(8, 64, range(1104, 1136))
    put(9, 0, range(976, 1104))
    for i in range(4):
        put(10 + i, 0, range(1136 + 128 * i, 1136 + 128 * (i + 1)))
    put(14, 0, range(1648, 1776)); put(15, 0, range(1776, 1904))
    put(16, 0, range(1904, 2032)); put(17, 0, range(2032, 2160))
    return m


def _gather_cols(w, cmap):
    out = np.zeros(w.shape[:-1] + (len(cmap),), dtype=np.float32)
    ok = cmap >= 0
    out[..., ok] = w[..., cmap[ok]]
    return out


def _fm(v):
    return np.ascontiguousarray(v.reshape(-1, 128).T)


def _prep_shared(inp):
    f = lambda k: np.asarray(inp[k], dtype=np.float32)
    sh = {}
    w_in = f("w_in")
    cmap = _win_colmap()
    win = _gather_cols(w_in, cmap)
    sh["win"] = np.ascontiguousarray(win.reshape(2, 8, 128, WIN_COLS).transpose(0, 2, 1, 3))
    sh["wout"] = np.ascontiguousarray(f("w_out").reshape(2, 8, 128, D).transpose(0, 2, 1, 3))
    wuq = f("mla_w_uq")
    qmap = -np.ones(768, dtype=np.int64)
    for h in range(4):
        qmap[(2 * h) * 96:(2 * h) * 96 + 96] = np.arange(h * 96, h * 96 + 96)
        qmap[(2 * h + 1) * 96 + 64:(2 * h + 1) * 96 + 96] = [h * 96 + 64 + (r + 16) % 32 for r in range(32)]
    wq = _gather_cols(wuq, qmap)
    wq_p = np.zeros((2, 256, 768), np.float32); wq_p[:, :192] = wq
    sh["wuq"] = np.ascontiguousarray(wq_p.reshape(2, 2, 128, 768).transpose(0, 2, 1, 3))
    wukv = f("mla_w_ukv")
    kvmap = np.zeros(768, dtype=np.int64)
    for h in range(4):
        kvmap[h * 64:(h + 1) * 64] = np.arange(h * 192, h * 192 + 64)
        kvmap[256 + h * 128:256 + (h + 1) * 128] = np.arange(h * 192 + 64, h * 192 + 192)
    sh["wukv"] = np.ascontiguousarray(wukv[..., kvmap])
    sh["wgate"] = np.ascontiguousarray(f("gla_w_gate"))
    sh["wglu"] = np.ascontiguousarray(f("s5_w_glu").reshape(2, 2, 128, 256).transpose(0, 2, 1, 3))
    wB = np.zeros((2, 2, 128, 8, 128), np.float32)
    wC = np.zeros((2, 2, 128, 8, 128), np.float32)
    Bs = (f("s5_b_re"), f("s5_b_im"))
    Cs = (f("s5_c_re"), f("s5_c_im"))
    for sg in range(8):
        for gl in range(2):
            g = 2 * sg + gl
            r0 = (g % 8) * 16
            for ri in range(2):
                wB[:, ri, r0:r0 + 16, sg, gl * 64:(gl + 1) * 64] = Bs[ri][:, g].transpose(0, 2, 1)
                wC[:, ri, gl * 64:(gl + 1) * 64, sg, r0:r0 + 16] = Cs[ri][:, g].transpose(0, 2, 1)
    sh["wB"] = wB; sh["wC"] = wC
    dD = np.zeros((2, 128, 2, 128), np.float32)
    d = f("s5_d")
    for uf in range(2):
        dD[:, np.arange(128), uf, np.arange(128)] = d[:, uf * 128:(uf + 1) * 128]
    sh["dD"] = dD
    vec = np.zeros((128, NV), np.float32)
    lre, lim, ldt = f("s5_lambda_re"), f("s5_lambda_im"), f("s5_log_dt")
    for l in range(2):
        b = l * VL
        vec[:, b:b + 8] = _fm(f("ln_gain")[l])
        qg = np.zeros(256, np.float32); qg[:192] = f("mla_q_norm_gain")[l]
        vec[:, b + 8:b + 10] = _fm(qg)
        vec[:, b + 10] = f("mla_kv_norm_gain")[l]
        vec[:, b + 11] = f("gla_b_gate")[l]
        vec[:, b + 12] = np.tile(f("gla_norm_gain")[l], 2)
        vec[:, b + 13:b + 15] = _fm(f("s5_b_glu")[l])
        for sg in range(8):
            for gl in range(2):
                g = 2 * sg + gl
                vec[gl * 64:(gl + 1) * 64, b + 15 + sg] = lre[l, g]
                vec[gl * 64:(gl + 1) * 64, b + 23 + sg] = lim[l, g]
                vec[gl * 64:(gl + 1) * 64, b + 31 + sg] = ldt[l, g]
    vec[:, 2 * VL:2 * VL + 8] = _fm(f("final_gain"))
    sh["vecs"] = vec
    c = np.zeros((128, NCONST), np.float32)
    c[:, C_ID:C_ID + 128] = np.eye(128)
    c[:, C_ONES:C_ONES + 128] = 1.0
    c[0:64, C_BLK:C_BLK + 64] = 1.0; c[64:128, C_BLK + 64:C_BLK + 128] = 1.0
    jj = np.arange(64)[:, None]; ii = np.arange(64)[None, :]
    c[0:64, C_CM64:C_CM64 + 64] = (ii >= jj)
    c[0:16, C_CM16:C_CM16 + 16] = (ii[:, :16] >= jj[:16])
    c[:, C_GM64:C_GM64 + 256] = (np.arange(256) % 64 != 0)[None, :]
    c[:, C_GM16:C_GM16 + 64] = (np.arange(64) % 16 != 0)[None, :]
    c[:, C_HM:C_HM + 4] = (np.arange(128)[:, None] // 32 == np.arange(4)[None, :])
    sh["consts"] = c
    pos = np.concatenate([np.arange(SEQ), np.tile(PAST + np.arange(DEC_SEQ), SPC)]).astype(np.float32)
    inv = (np.float32(10000.0) ** (-np.arange(16, dtype=np.float32) / np.float32(16))).astype(np.float32)
    ang = (pos[None, :] * inv[:, None]).astype(np.float32)
    rope = np.zeros((2, 128, NTOK), np.float32)
    rope[0, 64:80] = np.cos(ang); rope[0, 80:96] = np.cos(ang)
    rope[1, 64:80] = -np.sin(ang); rope[1, 80:96] = np.sin(ang)
    sh["rope"] = rope
    return sh


def _prep_core(inp, c):
    f = lambda k: np.asarray(inp[k], dtype=np.float32)
    pc = {}
    xa = np.concatenate([f("x_prompt")[c], f("x_sample")[SPC * c:SPC * (c + 1)].reshape(NS, D)], axis=0)
    pc["xT"] = np.ascontiguousarray(xa.T.reshape(8, 128, NTOK).transpose(1, 0, 2))
    pc["gla0"] = np.ascontiguousarray(f("state_gla")[:, SPC * c:SPC * (c + 1)].reshape(2, SPC, 128, 64))
    s5 = np.zeros((2, 2, SPC, 128, 8), np.float32)
    for ri, k in enumerate(("state_s5_re", "state_s5_im")):
        st = f(k)[:, SPC * c:SPC * (c + 1)]
        s5[:, ri] = st.reshape(2, SPC, 8, 2, 64).transpose(0, 1, 3, 4, 2).reshape(2, SPC, 128, 8)
    pc["s5x0"] = s5
    pc["ckvP"] = np.ascontiguousarray(f("cache_mla_ckv")[:, SPC * c:SPC * (c + 1)].transpose(0, 1, 3, 2))
    pc["kpeP"] = np.ascontiguousarray(f("cache_mla_kpe")[:, SPC * c:SPC * (c + 1)].transpose(0, 1, 3, 2))
    return pc


_CACHE = {}


def kernel(**inputs):
    if "prog" not in _CACHE:
        _CACHE["prog"] = build_program()
    nc, es, stats = _CACHE["prog"]
    sh = _prep_shared(inputs)
    in_maps = []
    for c in range(N_CORES):
        m = dict(sh)
        m.update(_prep_core(inputs, c))
        in_maps.append(m)
    res = run_bass_kernel_spmd(nc, in_maps, core_ids=list(range(N_CORES)))
    B = N_CORES
    y_p = np.zeros((B, SEQ, D), np.float32); y_s = np.zeros((B * SPC, DEC_SEQ, D), np.float32)
    gla_p = np.zeros((2, B, 4, 32, 64), np.float32); gla_s = np.zeros((2, B * SPC, 4, 32, 64), np.float32)
    ckv_p = np.zeros((2, B, SEQ, 128), np.float32); ckv_s = np.zeros((2, B * SPC, DEC_SEQ, 128), np.float32)
    kpe_p = np.zeros((2, B, SEQ, 32), np.float32); kpe_s = np.zeros((2, B * SPC, DEC_SEQ, 32), np.float32)
    re_p = np.zeros((2, B, 16, 64), np.float32); im_p = np.zeros((2, B, 16, 64), np.float32)
    re_s = np.zeros((2, B * SPC, 16, 64), np.float32); im_s = np.zeros((2, B * SPC, 16, 64), np.float32)

    def unstate(a):
        return a.reshape(2, 64, 8).transpose(2, 0, 1).reshape(16, 64)
    for c in range(N_CORES):
        r = res.results[c]
        ya = np.asarray(r["yT"]).transpose(1, 0, 2).reshape(D, NTOK).T
        y_p[c] = ya[:SEQ]; y_s[SPC * c:SPC * (c + 1)] = ya[SEQ:].reshape(SPC, DEC_SEQ, D)
        g = np.asarray(r["glaO"])
        gla_p[:, c] = g[:, 0].reshape(2, 4, 32, 64)
        gla_s[:, SPC * c:SPC * (c + 1)] = g[:, 1:].reshape(2, SPC, 4, 32, 64)
        ck = np.asarray(r["ckvO"]); kp = np.asarray(r["kpeO"])
        ckv_p[:, c] = ck[:, :, :SEQ].transpose(0, 2, 1)
        kpe_p[:, c] = kp[:, :, :SEQ].transpose(0, 2, 1)
        ckv_s[:, SPC * c:SPC * (c + 1)] = ck[:, :, SEQ:].reshape(2, 128, SPC, DEC_SEQ).transpose(0, 2, 3, 1)
        kpe_s[:, SPC * c:SPC * (c + 1)] = kp[:, :, SEQ:].reshape(2, 32, SPC, DEC_SEQ).transpose(0, 2, 3, 1)
        s5 = np.asarray(r["s5O"])
        for l in range(2):
            re_p[l, c] = unstate(s5[l, 0, 0]); im_p[l, c] = unstate(s5[l, 1, 0])
            for q in range(SPC):
                re_s[l, SPC * c + q] = unstate(s5[l, 0, 1 + q]); im_s[l, SPC * c + q] = unstate(s5[l, 1, 1 + q])
    return (y_p, y_s, gla_p, ckv_p, kpe_p, re_p, im_p, gla_s, ckv_s, kpe_s, re_s, im_s)
```
